# Optimizing a Trainium2 kernel written in Bass

```python
import functools
import jax, jax.numpy as jnp
from jax import lax
import numpy as np

D_MODEL = 2048
BATCH = 4
SEQ = 2048
DEPTH = 4
DEC_BATCH = 128
DEC_SEQ = 8
PAST_LEN = 16384
PAGE_SIZE = 128

N_META = 16
D_MIX = D_MODEL
D_CONV = D_MIX // 2
CONV_WIDTH = 3
CONV_GROUPS = 8
GLA_HEADS = 4
GLA_DV = (D_MIX - D_CONV) // GLA_HEADS
GLA_DK = GLA_DV // 2
GLA_GATE_RANK = 16
GLA_GATE_TAU = 16.0
GLA_CHUNK = 64
D_FF = 5632
EPS = 1e-6

MIX_IN_SIZES = (D_CONV, D_CONV, D_CONV, GLA_HEADS * GLA_DK, GLA_HEADS * GLA_DK,
                GLA_HEADS * GLA_DV, GLA_HEADS * GLA_DV, GLA_GATE_RANK)
D_MIX_IN = sum(MIX_IN_SIZES)
MIX_IN_OFFSETS = tuple(int(o) for o in np.cumsum(MIX_IN_SIZES)[:-1])

kernel_name = 'hybrid_conv_gla_macaron_step'


def rmsnorm(x, g):
    xf = x.astype(jnp.float32)
    y = xf * lax.rsqrt(jnp.mean(xf * xf, axis=-1, keepdims=True) + EPS)
    return (y * g.astype(jnp.float32)).astype(x.dtype)


def group_rmsnorm(x, g, groups):
    xf = x.astype(jnp.float32)
    xg = xf.reshape(x.shape[:-1] + (groups, x.shape[-1] // groups))
    yg = xg * lax.rsqrt(jnp.mean(xg * xg, axis=-1, keepdims=True) + EPS)
    return (yg.reshape(x.shape) * g.astype(jnp.float32)).astype(x.dtype)


def swiglu(x, w_gu, w_down):
    a, b = jnp.split(x @ w_gu, 2, axis=-1)
    return (jax.nn.silu(a) * b) @ w_down


def gla_chunk(S, q, k, v, g):
    L = q.shape[1]
    b = jnp.cumsum(g, axis=1)
    o_inter = jnp.einsum('blhd,bhdv->blhv', q * jnp.exp(b), S)
    causal = jnp.tril(jnp.ones((L, L), dtype=bool))[None, :, :, None, None]
    decay = jnp.exp(jnp.where(causal, b[:, :, None] - b[:, None, :], -jnp.inf))
    scores = jnp.einsum('bthd,bshd,btshd->bhts', q, k, decay)
    o_intra = jnp.einsum('bhts,bshv->bthv', scores, v)
    b_last = b[:, -1]
    S_new = jnp.exp(b_last)[..., None] * S + jnp.einsum(
        'bshd,bshv->bhdv', k * jnp.exp(b_last[:, None] - b), v)
    return S_new, o_inter + o_intra


def gla_prompt(q, k, v, g):
    Bsz, T = q.shape[:2]
    S0 = jnp.zeros((Bsz, GLA_HEADS, GLA_DK, GLA_DV), jnp.float32)
    S1, o_meta = gla_chunk(S0, q[:, :N_META], k[:, :N_META], v[:, :N_META], g[:, :N_META])
    n_chunks = (T - N_META) // GLA_CHUNK

    def to_chunks(a):
        return a[:, N_META:].reshape((Bsz, n_chunks, GLA_CHUNK) + a.shape[2:]).swapaxes(0, 1)

    S_fin, o_rest = lax.scan(lambda S, xs: gla_chunk(S, *xs), S1,
                             (to_chunks(q), to_chunks(k), to_chunks(v), to_chunks(g)))
    o_rest = o_rest.swapaxes(0, 1).reshape((Bsz, T - N_META) + o_rest.shape[3:])
    return S_fin, jnp.concatenate([o_meta, o_rest], axis=1)


def token_mixer(xn, conv_prev, gla_fn, w_in, conv_w, conv_g, fw2, fb, gla_g, w_out):
    Bsz, T, _ = xn.shape
    f32 = jnp.float32
    cB, cC, ch, q, k, v, go, fl = jnp.split(xn @ w_in, MIX_IN_OFFSETS, axis=-1)
    u = cC * ch
    pad = jnp.concatenate([conv_prev.astype(u.dtype), u], axis=1)
    conv = sum(conv_w[j] * pad[:, j:j + T] for j in range(CONV_WIDTH))
    yc = group_rmsnorm(cB * conv, conv_g, CONV_GROUPS)
    new_conv = pad[:, T:]
    q = q.astype(f32).reshape(Bsz, T, GLA_HEADS, GLA_DK) * (GLA_DK ** -0.5)
    k = k.astype(f32).reshape(Bsz, T, GLA_HEADS, GLA_DK)
    v = v.astype(f32).reshape(Bsz, T, GLA_HEADS, GLA_DV)
    glog = (jax.nn.log_sigmoid((fl @ fw2 + fb).astype(f32)) / GLA_GATE_TAU).reshape(
        Bsz, T, GLA_HEADS, GLA_DK)
    S_new, o = gla_fn(q, k, v, glog)
    o = rmsnorm(o, gla_g) * jax.nn.silu(go.astype(f32).reshape(Bsz, T, GLA_HEADS, GLA_DV))
    yg = o.reshape(Bsz, T, GLA_HEADS * GLA_DV).astype(xn.dtype)
    y = jnp.concatenate([yc, yg], axis=-1) @ w_out
    return y, new_conv, S_new


def decoder_layer(h, conv_prev, gla_fn, p):
    (n1, gu1, dn1, nm, w_in, cw, cg, fw2, fb, gg, w_out, n2, gu2, dn2) = p
    h = h + 0.5 * swiglu(rmsnorm(h, n1), gu1, dn1)
    y, new_conv, S_new = token_mixer(rmsnorm(h, nm), conv_prev, gla_fn, w_in, cw, cg, fw2, fb, gg, w_out)
    h = h + y
    h = h + 0.5 * swiglu(rmsnorm(h, n2), gu2, dn2)
    return h, new_conv, S_new


def setup_inputs(seed: int = 0) -> dict:
    key = jax.random.key(seed)
    ks = jax.random.split(key, 20)

    def nrm(k, shape, scale):
        return jax.random.normal(k, shape, jnp.float32) * scale

    def gain(k, shape):
        return 1.0 + nrm(k, shape, 0.02)

    return {
        'x_prompt': nrm(ks[0], (BATCH, SEQ, D_MODEL), 1.0),
        'x_sample': nrm(ks[1], (DEC_BATCH, DEC_SEQ, D_MODEL), 1.0),
        'state_conv': nrm(ks[2], (DEPTH, DEC_BATCH, CONV_WIDTH - 1, D_CONV), 1.0),
        'state_gla': nrm(ks[3], (DEPTH, DEC_BATCH, GLA_HEADS, GLA_DK, GLA_DV), 0.5),
        'meta_tokens': nrm(ks[4], (N_META, D_MODEL), 1.0),
        'norm_ffn1': gain(ks[5], (DEPTH, D_MODEL)),
        'w_ffn1_gu': nrm(ks[6], (DEPTH, D_MODEL, 2 * D_FF), D_MODEL ** -0.5),
        'w_ffn1_down': nrm(ks[7], (DEPTH, D_FF, D_MODEL), D_FF ** -0.5),
        'norm_mix': gain(ks[8], (DEPTH, D_MODEL)),
        'w_mix_in': nrm(ks[9], (DEPTH, D_MODEL, D_MIX_IN), D_MODEL ** -0.5),
        'conv_w': nrm(ks[10], (DEPTH, CONV_WIDTH, D_CONV), CONV_WIDTH ** -0.5),
        'conv_norm': gain(ks[11], (DEPTH, D_CONV)),
        'gla_fgate_w2': nrm(ks[12], (DEPTH, GLA_GATE_RANK, GLA_HEADS * GLA_DK), GLA_GATE_RANK ** -0.5),
        'gla_fgate_b': nrm(ks[13], (DEPTH, GLA_HEADS * GLA_DK), 0.1),
        'gla_out_norm': gain(ks[14], (DEPTH, GLA_DV)),
        'w_mix_out': nrm(ks[15], (DEPTH, D_MIX, D_MODEL), D_MIX ** -0.5),
        'norm_ffn2': gain(ks[16], (DEPTH, D_MODEL)),
        'w_ffn2_gu': nrm(ks[17], (DEPTH, D_MODEL, 2 * D_FF), D_MODEL ** -0.5),
        'w_ffn2_down': nrm(ks[18], (DEPTH, D_FF, D_MODEL), D_FF ** -0.5),
        'norm_final': gain(ks[19], (D_MODEL,)),
    }


def reference(x_prompt, x_sample, state_conv, state_gla, meta_tokens, norm_ffn1, w_ffn1_gu,
              w_ffn1_down, norm_mix, w_mix_in, conv_w, conv_norm, gla_fgate_w2, gla_fgate_b,
              gla_out_norm, w_mix_out, norm_ffn2, w_ffn2_gu, w_ffn2_down, norm_final):
    bp = x_prompt.shape[0]
    meta = jnp.broadcast_to(meta_tokens.astype(x_prompt.dtype)[None], (bp, N_META, D_MODEL))
    hp = jnp.concatenate([meta, x_prompt], axis=1)
    hs = x_sample
    conv_p, gla_p, conv_s, gla_s = [], [], [], []
    for l in range(DEPTH):
        p = (norm_ffn1[l], w_ffn1_gu[l], w_ffn1_down[l], norm_mix[l], w_mix_in[l], conv_w[l],
             conv_norm[l], gla_fgate_w2[l], gla_fgate_b[l], gla_out_norm[l], w_mix_out[l],
             norm_ffn2[l], w_ffn2_gu[l], w_ffn2_down[l])
        zeros_conv = jnp.zeros((bp, CONV_WIDTH - 1, D_CONV), hp.dtype)
        hp, c_new, S_new = decoder_layer(hp, zeros_conv, gla_prompt, p)
        conv_p.append(c_new)
        gla_p.append(S_new)
        gla_fn_s = functools.partial(gla_chunk, state_gla[l].astype(jnp.float32))
        hs, c_new, S_new = decoder_layer(hs, state_conv[l], gla_fn_s, p)
        conv_s.append(c_new)
        gla_s.append(S_new)
    y_prompt = rmsnorm(hp, norm_final)[:, N_META:]
    y_sample = rmsnorm(hs, norm_final)
    new_gla_prompt = jnp.stack(gla_p)
    new_conv_prompt = jnp.stack(conv_p)
    new_gla_sample = jnp.stack(gla_s)
    new_conv_sample = jnp.stack(conv_s)
    return (y_prompt, y_sample, new_gla_prompt, new_conv_prompt, new_gla_sample, new_conv_sample)
```

```python
import contextlib
import numpy as np
import concourse.bass as bass
import concourse.mybir as mybir
from concourse.bass_utils import run_bass_kernel_spmd

F32 = mybir.dt.float32
BF16 = mybir.dt.bfloat16
AF = mybir.ActivationFunctionType
ALU = mybir.AluOpType

D = 2048
DFF = 5632
DEPTH = 4
NPR = 1032
NS = 128
NT = NPR + NS
TT = [(0, 392), (392, 384), (776, 384)]
PCH = [(i * 128, 128) for i in range(8)] + [(1024, 8)]
ACH = PCH + [(NPR, 128)]
XCH = [(i * 128, 128) for i in range(9)] + [(1152, 8)]
EPS = 1e-6
DK = 128
NVL = 86
NV = NVL * DEPTH + 16
NSLOT = 3
WSPEC = [("wgu1", D, 2 * DFF), ("wdn1", DFF, D), ("win", D, 6160), ("wout", D, D), ("wgu2", D, 2 * DFF), ("wdn2", DFF, D)]
ENGS = ("pe", "act", "dve", "pool", "sp")


class Buf:
    __slots__ = ("name", "w", "r")

    def __init__(self, name=""):
        self.name = name
        self.w = None
        self.r = []


class Prog:
    def __init__(self, nc):
        self.nc = nc
        self.q = {e: [] for e in ENGS}
        self.cnt = {e: 0 for e in ENGS}
        self.waited = {e: {} for e in ENGS}
        self.dma_sems = {}
        self.rr = {}

    def _waits_for(self, reads, writes):
        toks = []
        for b in reads:
            if b.w is not None:
                toks.append(b.w)
        for b in writes:
            if b.w is not None:
                toks.append(b.w)
            toks.extend(b.r)
        return toks

    def _filter(self, eng, toks):
        out = {}
        wd = self.waited[eng]
        for (s, v) in toks:
            if wd.get(s, 0) >= v:
                continue
            if out.get(s, 0) < v:
                out[s] = v
        for s, v in out.items():
            wd[s] = v
        return list(out.items())

    def _register(self, tok, reads, writes):
        for b in reads:
            b.r.append(tok)
        for b in writes:
            b.w = tok
            b.r = []

    def op(self, eng, fn, reads=(), writes=(), sem=None):
        toks = self._waits_for(reads, writes)
        waits = self._filter(eng, toks)
        sname = sem if sem is not None else eng
        if sem is not None:
            self.dma_sems.setdefault(sem, 0)
        self.cnt[sname] = self.cnt.get(sname, 0) + 1
        tok = (sname, self.cnt[sname])
        self.q[eng].append((waits, fn, (sname, 1)))
        self._register(tok, reads, writes)
        return tok

    def dma(self, queue, sem, out_ap, in_ap, reads=(), writes=()):
        n = self.dma_sems.get(sem, 0)
        toks = self._waits_for(reads, writes)
        if n > 0:
            toks.append((sem, 16 * n))
        waits = self._filter(queue, toks)
        self.dma_sems[sem] = n + 1
        tok = (sem, 16 * (n + 1))

        def fn(e, out_ap=out_ap, in_ap=in_ap):
            return e.dma_start(out=out_ap, in_=in_ap)
        self.q[queue].append((waits, fn, (sem, 16)))
        self._register(tok, reads, writes)
        return tok

    def dma_rr(self, queue, pool, n, out_ap, in_ap, reads=(), writes=()):
        i = self.rr.get(pool, 0)
        self.rr[pool] = i + 1
        return self.dma(queue, "%s%d" % (pool, i % n), out_ap, in_ap, reads, writes)

    def wait_all(self, eng, toks):
        waits = self._filter(eng, toks)
        self.q[eng].append((waits, None, None))

    def build(self):
        nc = self.nc
        names = list(ENGS) + sorted(self.dma_sems.keys())
        with contextlib.ExitStack() as st:
            sems = {}
            for n in names:
                sems[n] = st.enter_context(nc.semaphore("s_" + n))
            block = st.enter_context(nc.Block())

            def run(e, ename):
                for waits, fn, inc in self.q[ename]:
                    for (s, v) in waits:
                        e.wait_ge(sems[s], v)
                    if fn is not None:
                        ins = fn(e)
                        ins.then_inc(sems[inc[0]], inc[1])

            @block.tensor
            def _(e):
                run(e, "pe")

            @block.scalar
            def _(e):
                run(e, "act")

            @block.vector
            def _(e):
                run(e, "dve")

            @block.gpsimd
            def _(e):
                run(e, "pool")

            @block.sync
            def _(e):
                run(e, "sp")


def build_program(depth=DEPTH, dbg=False):
    nc = bass.Bass("TRN2", target_bir_lowering=False)

    def din(name, shape):
        return nc.dram_tensor(name, list(shape), F32, kind="ExternalInput").ap()

    def dout(name, shape):
        return nc.dram_tensor(name, list(shape), F32, kind="ExternalOutput").ap()

    x_in = din("x_in", [NT, D])
    vecs_d = din("vecs", [128, NV])
    role_d = din("role", [128, 8])
    fw2_d = din("fw2", [DEPTH, 16, 512])
    sconv_d = din("sconv", [DEPTH, 32, 1024])
    sgla_d = din("sgla", [DEPTH, 16, 4, 128, 256])
    w_ext, w_full, b_wfull = {}, {}, {}
    for name, Rr, Cc in WSPEC:
        w_ext[name] = din(name, [depth, Rr, Cc])
        w_full[name] = [w_ext[name][l] for l in range(depth)]
        b_wfull[name] = [Buf() for l in range(depth)]
    wgu = ["wgu1", "wgu2"]
    wdn = ["wdn1", "wdn2"]

    y_out = dout("y_out", [NT, D])
    gla_p = dout("gla_p", [DEPTH, 4, 128, 256])
    conv_p = dout("conv_p", [DEPTH, 2, 1024])
    gla_s = dout("gla_s", [DEPTH, 16, 4, 128, 256])
    conv_s = dout("conv_s", [DEPTH, 32, 1024])
    if dbg:
        dbg_h = dout("dbg_h", [128, 16, NT])

    cc_src = [[nc.dram_tensor("cc_src_%d_%d" % (l, hd), [128, 272], F32).ap() for hd in range(4)] for l in range(depth)]
    cc_dst = [[nc.dram_tensor("cc_dst_%d_%d" % (l, hd), [256, 272], F32).ap() for hd in range(4)] for l in range(depth)]

    es = contextlib.ExitStack()
    with es:
        def sb(name, shape, dt):
            return es.enter_context(nc.sbuf_tensor(name, list(shape), dt))

        def ps(name, shape, dt=F32):
            return es.enter_context(nc.psum_tensor(name, list(shape), dt))

        h = sb("h", [128, 16, NT], F32)
        xn = sb("xn", [128, 16, NT], BF16)
        vecs = sb("vecs_sb", [128, NV], F32)
        negfb = sb("negfb", [128, 16], F32)
        role = sb("role_sb", [128, 8], F32)
        cst = sb("cst", [128, 4], F32)
        ident = sb("ident", [128, 128], F32)
        ident_bf = sb("ident_bf", [128, 128], BF16)
        ones_bf = sb("ones_bf", [128, 128], BF16)
        ones_f = sb("ones_f", [128, 2], F32)
        maskU = sb("maskU", [128, 128], F32)
        maskBD = sb("maskBD", [128, 128], BF16)
        MS = sb("MS", [128, 16], BF16)
        MSf = sb("MSf", [128, 16], F32)
        MSTf = sb("MSTf", [16, 128], F32)
        MST = sb("MST", [16, 128], BF16)
        fw2 = sb("fw2_sb", [16, 512], F32)
        rstd = sb("rstd", [128, NT], F32)
        sq0 = sb("sq0", [128, NT], BF16)
        sq = [sq0, sq0]
        wring = [sb("wring%d" % i, [128, 16, 256], BF16) for i in range(NSLOT)]
        N16 = 18512
        N32 = 5400
        scr16 = sb("scr16", [128, N16], BF16)
        scr32 = sb("scr32", [128, N32], F32)

        psA = [ps("psA%d" % i, [128, 512]) for i in range(3)]
        psB = [ps("psB%d" % i, [128, 512]) for i in range(3)]
        T1 = ps("T1", [128, 512])
        T2 = ps("T2", [128, 512])

        P = Prog(nc)

        b_h = [Buf("h%d" % k) for k in range(16)]
        b_xn = [Buf("xn%d" % k) for k in range(16)]
        b_vecs, b_negfb, b_role, b_cst = Buf(), Buf(), Buf(), Buf()
        b_ident, b_identbf, b_ones, b_onesf = Buf(), Buf(), Buf(), Buf()
        b_maskU, b_maskBD, b_MS, b_MSf, b_MSTf, b_MST, b_fw2 = Buf(), Buf(), Buf(), Buf(), Buf(), Buf(), Buf()
        b_rstd = Buf()
        b_sq0 = Buf()
        b_sq = [b_sq0, b_sq0]
        b_w = [Buf("w%d" % i) for i in range(NSLOT)]
        b_psA = [Buf() for _ in range(3)]
        b_psB = [Buf() for _ in range(3)]
        b_T1a, b_T1b, b_T2a, b_T2b = Buf(), Buf(), Buf(), Buf()

        out_toks = {}

        def note_out(tok):
            out_toks[tok[0]] = max(out_toks.get(tok[0], 0), tok[1])

        P.op("pool", lambda e: e.memset(ident[:], 0.0), writes=[b_ident])
        P.op("pool", lambda e: e.affine_select(out=ident[:], in_=ident[:], pattern=[[-1, 128]],
                                               compare_op=ALU.not_equal, fill=1.0, base=0, channel_multiplier=1),
             reads=[b_ident], writes=[b_ident])
        P.op("pool", lambda e: e.memset(ones_bf[:], 1.0), writes=[b_ones])
        P.op("pool", lambda e: e.memset(ones_f[:], 1.0), writes=[b_onesf])
        P.op("pool", lambda e: e.memset(cst[:, 0:1], EPS), writes=[b_cst])
        P.op("pool", lambda e: e.memset(cst[:, 1:2], 1.0), reads=[b_cst], writes=[b_cst])
        P.op("pool", lambda e: e.memset(maskU[:], 1.0), writes=[b_maskU])
        P.op("pool", lambda e: e.affine_select(out=maskU[:], in_=maskU[:], pattern=[[1, 128]],
                                               compare_op=ALU.is_ge, fill=0.0, base=0, channel_multiplier=-1),
             reads=[b_maskU], writes=[b_maskU])
        P.op("pool", lambda e: e.memset(MSf[:], 1.0), writes=[b_MSf])
        P.op("pool", lambda e: e.affine_select(out=MSf[:], in_=MSf[:], pattern=[[-8, 16]],
                                               compare_op=ALU.is_ge, fill=0.0, base=0, channel_multiplier=1),
             reads=[b_MSf], writes=[b_MSf])
        P.op("pool", lambda e: e.affine_select(out=MSf[:], in_=MSf[:], pattern=[[8, 16]],
                                               compare_op=ALU.is_ge, fill=0.0, base=7, channel_multiplier=-1),
             reads=[b_MSf], writes=[b_MSf])
        P.op("pool", lambda e: e.memset(MSTf[:], 1.0), writes=[b_MSTf])
        P.op("pool", lambda e: e.affine_select(out=MSTf[:], in_=MSTf[:], pattern=[[1, 128]],
                                               compare_op=ALU.is_ge, fill=0.0, base=0, channel_multiplier=-8),
             reads=[b_MSTf], writes=[b_MSTf])
        P.op("pool", lambda e: e.affine_select(out=MSTf[:], in_=MSTf[:], pattern=[[-1, 128]],
                                               compare_op=ALU.is_ge, fill=0.0, base=7, channel_multiplier=8),
             reads=[b_MSTf], writes=[b_MSTf])
        P.op("dve", lambda e: e.tensor_copy(out=ident_bf[:], in_=ident[:]), reads=[b_ident], writes=[b_identbf])
        P.op("dve", lambda e: e.tensor_copy(out=MS[:], in_=MSf[:]), reads=[b_MSf], writes=[b_MS])
        P.op("dve", lambda e: e.tensor_copy(out=MST[:], in_=MSTf[:]), reads=[b_MSTf], writes=[b_MST])
        P.op("pe", lambda e: e.matmul(T1[:, 0:128], lhsT=MST[0:16, :], rhs=MST[0:16, :], start=True, stop=True),
             reads=[b_MST], writes=[b_T1a])
        P.op("dve", lambda e: e.tensor_tensor(out=maskBD[:], in0=T1[:, 0:128], in1=maskU[:], op=ALU.mult),
             reads=[b_T1a, b_maskU], writes=[b_maskBD])

        P.dma("sp", "ldm", vecs[:], vecs_d, writes=[b_vecs])
        P.dma("sp", "ldm", role[:], role_d, writes=[b_role])
        for l in range(DEPTH):
            P.op("dve", lambda e, l=l: e.tensor_scalar(out=negfb[:, 4 * l:4 * l + 4],
                                                      in0=vecs[:, NVL * l + 80:NVL * l + 84],
                                                      scalar1=-1.0, scalar2=None, op0=ALU.mult),
                 reads=[b_vecs], writes=[b_negfb])

        ring_i = [0]

        def load_w(name, l, r0, nk, c0, ncols):
            s = ring_i[0] % NSLOT
            ring_i[0] += 1
            src = w_full[name][l][r0:r0 + nk * 128, c0:c0 + ncols]
            P.dma("pool", "dw%d" % s, wring[s][:, 0:nk, 0:ncols],
                  src.rearrange("(k p) m -> p k m", p=128), reads=[b_wfull[name][l]], writes=[b_w[s]])
            return s

        def gather(names, l):
            return

        def proj_fm(pset, b_pset, slot, col, nk, rhs_fn, rhs_bufs, m=128):
            def f(e):
                ins = None
                for k in range(nk):
                    for t, (t0, sz) in enumerate(TT):
                        ins = e.matmul(pset[t][0:m, 0:sz], lhsT=wring[slot][:, k, col:col + m],
                                       rhs=rhs_fn(k, t0, sz), start=(k == 0), stop=(k == nk - 1))
                return ins
            return P.op("pe", f, reads=[b_w[slot]] + list(rhs_bufs), writes=list(b_pset))

        def stats_to_rstd(pset, b_pset, inv_n):
            for t, (t0, sz) in enumerate(TT):
                P.op("act", lambda e, t=t, t0=t0, sz=sz: e.activation(
                    out=rstd[:, t0:t0 + sz], in_=pset[t][:, 0:sz], func=AF.Ln, scale=inv_n, bias=cst[:, 0:1]),
                    reads=[b_pset[t], b_cst], writes=[b_rstd])
            P.op("act", lambda e: e.activation(out=rstd[:], in_=rstd[:], func=AF.Exp, scale=-0.5),
                 reads=[b_rstd], writes=[b_rstd])

        def ones_mm(pset, b_pset, src, b_src, first, last):
            def f(e):
                ins = None
                for t, (t0, sz) in enumerate(TT):
                    ins = e.matmul(pset[t][:, 0:sz], lhsT=ones_bf[:], rhs=src[:, t0:t0 + sz], start=first, stop=last)
                return ins
            return P.op("pe", f, reads=[b_src, b_ones], writes=list(b_pset))

        def h_stats():
            for kc in range(16):
                i = kc % 2
                P.op("act", lambda e, kc=kc, i=i: e.activation(out=sq[i][:], in_=h[:, kc, :], func=AF.Square),
                     reads=[b_h[kc]], writes=[b_sq[i]])
                ones_mm(psA, b_psA, sq[i], b_sq[i], kc == 0, kc == 15)
            stats_to_rstd(psA, b_psA, 1.0 / D)

        def rmsnorm_xn(gcol0):
            h_stats()
            for kc in range(16):
                P.op("dve", lambda e, kc=kc: e.scalar_tensor_tensor(
                    out=xn[:, kc, :], in0=h[:, kc, :], scalar=vecs[:, gcol0 + kc:gcol0 + kc + 1], in1=rstd[:],
                    op0=ALU.mult, op1=ALU.mult),
                    reads=[b_h[kc], b_vecs, b_rstd], writes=[b_xn[kc]])

        def xn_rhs(k, t0, sz):
            return xn[:, k, t0:t0 + sz]

        gather(["wgu1", "wdn1", "win", "wout"], 0)

        xtok = [scr32[:, 0:2048], scr32[:, 2048:4096]]
        b_xtok = [Buf(), Buf()]
        Tb = [(T1, [b_T1a, b_T1b]), (T2, [b_T2a, b_T2b])]
        gi = 0
        for c, (c0, n) in enumerate(XCH):
            s = c % 2
            P.dma("sp", "ldx%d" % s, xtok[s][0:n, :], x_in[c0:c0 + n, :], writes=[b_xtok[s]])
            for g in range(4):
                Tt, bT = Tb[gi % 2]

                def f(e, s=s, g=g, n=n, Tt=Tt):
                    ins = None
                    for j in range(4):
                        kc = 4 * g + j
                        ins = e.transpose(Tt[:, j * 128:j * 128 + n], xtok[s][0:n, kc * 128:(kc + 1) * 128],
                                          ident[0:n, 0:n])
                    return ins
                P.op("pe", f, reads=[b_xtok[s], b_ident], writes=bT)
                src = Tt[:].rearrange("p (j m) -> p j m", j=4)[:, :, 0:n]
                dst = h[:, 4 * g:4 * g + 4, c0:c0 + n]
                if gi % 2 == 0:
                    P.op("dve", lambda e, src=src, dst=dst: e.tensor_copy(out=dst, in_=src),
                         reads=bT, writes=b_h[4 * g:4 * g + 4])
                else:
                    P.op("act", lambda e, src=src, dst=dst: e.activation(out=dst, in_=src, func=AF.Copy),
                         reads=bT, writes=b_h[4 * g:4 * g + 4])
                gi += 1

        hidden = scr16[:, 0:11 * NT].rearrange("p (j t) -> p j t", j=11)
        b_hid = [Buf() for _ in range(11)]
        sa = [scr32[:, 0:NT], scr32[:, NT:2 * NT]]
        b_sa = [Buf(), Buf()]

        def ffn(l, which, gcol0):
            W1 = wgu[which]
            W2 = wdn[which]
            barrier(G_all)
            if which == 0:
                gather(["wgu2", "wdn2"], l)
            elif l + 1 < depth:
                gather(["wgu1", "wdn1", "win", "wout"], l + 1)
            rmsnorm_xn(gcol0)
            jj = 0
            for q in range(4):
                for s6 in range(6):
                    nch = 2 if s6 < 5 else 1
                    c0 = 1408 * q + 256 * s6
                    sA = load_w(W1, l, 0, 16, c0, 128 * nch)
                    sB = load_w(W1, l, 0, 16, DFF + c0, 128 * nch)
                    for i in range(nch):
                        j = 2 * s6 + i
                        si = jj % 2
                        jj += 1
                        proj_fm(psA, b_psA, sA, 128 * i, 16, xn_rhs, b_xn)
                        for t, (t0, sz) in enumerate(TT):
                            P.op("act", lambda e, t=t, t0=t0, sz=sz, si=si: e.activation(
                                out=sa[si][:, t0:t0 + sz], in_=psA[t][:, 0:sz], func=AF.Silu),
                                reads=[b_psA[t]], writes=[b_sa[si]])
                        proj_fm(psB, b_psB, sB, 128 * i, 16, xn_rhs, b_xn)
                        for t, (t0, sz) in enumerate(TT):
                            P.op("dve", lambda e, t=t, t0=t0, sz=sz, si=si, j=j: e.tensor_tensor(
                                out=hidden[:, j, t0:t0 + sz], in0=psB[t][:, 0:sz], in1=sa[si][:, t0:t0 + sz],
                                op=ALU.mult),
                                reads=[b_psB[t], b_sa[si]], writes=[b_hid[j]])
                for blk in range(8):
                    sD = load_w(W2, l, 1408 * q, 11, 256 * blk, 256)
                    for i in range(2):
                        dc = 2 * blk + i
                        pset, bset = (psA, b_psA) if dc % 2 == 0 else (psB, b_psB)
                        proj_fm(pset, bset, sD, 128 * i, 11, lambda k, t0, sz: hidden[:, k, t0:t0 + sz], b_hid)
                        for t, (t0, sz) in enumerate(TT):
                            P.op("dve", lambda e, t=t, t0=t0, sz=sz, dc=dc, pset=pset: e.scalar_tensor_tensor(
                                out=h[:, dc, t0:t0 + sz], in0=pset[t][:, 0:sz], scalar=0.5, in1=h[:, dc, t0:t0 + sz],
                                op0=ALU.mult, op1=ALU.add),
                                reads=[bset[t], b_h[dc]], writes=[b_h[dc]])

        o16 = [0]
        o32 = [0]

        def c16(n):
            a = o16[0]
            o16[0] += n
            assert o16[0] <= N16, o16[0]
            return scr16[:, a:a + n]

        def c32(n):
            a = o32[0]
            o32[0] += n
            assert o32[0] <= N32, o32[0]
            return scr32[:, a:a + n]

        ymix8 = c16(8 * NT).rearrange("p (k t) -> p k t", k=8)
        b_ymix = [Buf() for _ in range(8)]
        qtl, ktl = c16(NT), c16(NT)
        khT = sq0
        b_qtl, b_ktl, b_khT = Buf(), Buf(), b_sq0
        khtok = c16(10 * 128).rearrange("p (c d) -> p c d", c=10)
        b_khtok = Buf()
        vtok = c16(10 * 256).rearrange("p (c d) -> p c d", c=10)
        b_vtok = Buf()
        S_bf = c16(256)
        b_Sbf = Buf()
        sc_sb = [c16(128), c16(128)]
        b_sc = [Buf(), Buf()]
        km = c16(16 * 128).rearrange("p (j d) -> p j d", j=16)
        b_km = Buf()
        Sj_bf = [c16(256), c16(256)]
        b_Sjbf = [Buf(), Buf()]
        mix16_end = o16[0]

        TTb = c32(2 * NT).rearrange("p (v t) -> p v t", v=2)
        b_TT = [Buf(), Buf()]
        S32 = c32(256)
        b_S = Buf()
        pay = c32(272)
        b_pay = Buf()
        sin = c32(272)
        b_sin = Buf()
        Sj = [c32(256), c32(256)]
        b_Sj = [Buf(), Buf()]
        Snew = [c32(256), c32(256)]
        b_Snew = [Buf(), Buf()]
        fl = c32(NT)
        b_fl = Buf()
        Elp = c32(16)
        Els = c32(16)
        b_El = Buf()
        halo_in = c32(16)
        b_halo = Buf()
        ul = c32(32).rearrange("p (g c r) -> p g c r", g=2, c=8)
        b_ul = Buf()
        gla32_end = o32[0]
        o32[0] = 0
        cC_sb = c32(NT)
        upad = c32(1194)
        cv = c32(1194)
        zt = c32(NT)
        b_cC, b_upad, b_cv, b_z = Buf(), Buf(), Buf(), Buf()
        sconvT = c32(256).rearrange("p (c r) -> p c r", c=8)
        b_sconvT = Buf()
        uo = c32(8 * 34).rearrange("p (c r) -> p c r", c=8)
        b_uo = Buf()
        conv32_end = o32[0]
        assert conv32_end <= 5304
        o32[0] = max(gla32_end, conv32_end)
        stg = sb("stg", [34, 512], F32)
        b_stg = Buf()
        G_gla32 = [b_TT[0], b_TT[1], b_S, b_pay, b_sin, b_Sj[0], b_Sj[1], b_Snew[0], b_Snew[1], b_fl]
        G_conv32 = [b_cC, b_upad, b_cv, b_z, b_sconvT, b_uo]
        G_mix16 = b_ymix + [b_qtl, b_ktl, b_khtok, b_vtok, b_Sbf, b_sc[0], b_sc[1], b_km, b_Sjbf[0], b_Sjbf[1]]
        G_ffn = b_hid + b_sa
        G_x = b_xtok
        G_all = G_gla32 + G_conv32 + G_mix16 + G_ffn + G_x + [b_El, b_halo, b_ul]

        def barrier(bufs):
            P.op("act", lambda e: e.activation(out=cst[:, 2:3], in_=cst[:, 1:2], func=AF.Copy),
                 reads=[b_cst], writes=list(bufs))

        def mixer(l):
            vb = NVL * l
            barrier(G_all)
            rmsnorm_xn(vb + 16)
            W = "win"
            P.dma("sp", "ldm", fw2[:], fw2_d[l], writes=[b_fw2])

            sF = load_w(W, l, 0, 16, 6144, 16)
            proj_fm(psA, b_psA, sF, 0, 16, xn_rhs, b_xn, m=16)
            for t, (t0, sz) in enumerate(TT):
                P.op("act", lambda e, t=t, t0=t0, sz=sz: e.activation(out=fl[0:16, t0:t0 + sz], in_=psA[t][0:16, 0:sz],
                                                                     func=AF.Copy),
                     reads=[b_psA[t]], writes=[b_fl])
            for g in range(2):
                for cp in range(4):
                    sU = load_w(W, l, 0, 16, 1024 * (g + 1) + 256 * cp, 256)
                    for i in range(2):
                        cc = 2 * cp + i

                        def f(e, sU=sU, i=i):
                            ins = None
                            for k in range(16):
                                ins = e.matmul(T2[:, 256:258], lhsT=wring[sU][:, k, 128 * i:128 * (i + 1)],
                                               rhs=xn[:, k, NPR - 2:NPR], start=(k == 0), stop=(k == 15))
                            return ins
                        P.op("pe", f, reads=[b_w[sU]] + b_xn, writes=[b_T2b])
                        P.op("act", lambda e, g=g, cc=cc: e.activation(out=ul[:, g, cc, :], in_=T2[:, 256:258], func=AF.Copy),
                             reads=[b_T2b], writes=[b_ul])
            P.op("dve", lambda e: e.tensor_tensor(out=pay[:, 256:272].rearrange("p (c r) -> p c r", c=8),
                                                  in0=ul[:, 0, :, :], in1=ul[:, 1, :, :], op=ALU.mult),
                 reads=[b_ul], writes=[b_pay])

            for hd in range(4):
                T1buf = TTb[:, 0, :]
                T2buf = TTb[:, 1, :]
                def f(e, hd=hd):
                    ins = None
                    for t, (t0, sz) in enumerate(TT):
                        ins = e.matmul(psA[t][:, 0:sz], lhsT=fw2[0:16, hd * 128:(hd + 1) * 128],
                                       rhs=fl[0:16, t0:t0 + sz], start=True, stop=True)
                    return ins
                P.op("pe", f, reads=[b_fl, b_fw2], writes=b_psA)
                for t, (t0, sz) in enumerate(TT):
                    P.op("act", lambda e, t=t, t0=t0, sz=sz, hd=hd: e.activation(
                        out=T1buf[:, t0:t0 + sz], in_=psA[t][:, 0:sz], func=AF.Exp, scale=-1.0,
                        bias=negfb[:, 4 * l + hd:4 * l + hd + 1]),
                        reads=[b_psA[t], b_negfb], writes=[b_TT[0]])
                P.op("act", lambda e: e.activation(out=T1buf, in_=T1buf, func=AF.Ln, bias=cst[:, 1:2]),
                     reads=[b_TT[0], b_cst], writes=[b_TT[0]])
                P.op("dve", lambda e: e.tensor_tensor_scan(out=T2buf, data0=T1buf, data1=T1buf, initial=0.0,
                                                           op0=ALU.add, op1=ALU.max),
                     reads=[b_TT[0]], writes=[b_TT[1]])
                P.op("dve", lambda e: e.tensor_copy(out=T1buf[:, 0:128], in_=T2buf[:, 0:128]),
                     reads=[b_TT[1]], writes=[b_TT[0]])
                for (s0, n) in PCH[1:]:
                    P.op("dve", lambda e, s0=s0, n=n: e.tensor_scalar(
                        out=T1buf[:, s0:s0 + n], in0=T2buf[:, s0:s0 + n], scalar1=T2buf[:, s0 - 1:s0], scalar2=None,
                        op0=ALU.subtract),
                        reads=[b_TT[1]], writes=[b_TT[0]])
                P.op("dve", lambda e: e.tensor_tensor(
                    out=T1buf[:, NPR:NT].rearrange("p (j i) -> p j i", i=8),
                    in0=T2buf[:, NPR:NT].rearrange("p (j i) -> p j i", i=8),
                    in1=T2buf[:, NPR - 1:NT - 1].rearrange("p (j i) -> p j i", i=8)[:, :, 0:1].to_broadcast([128, 16, 8]),
                    op=ALU.subtract),
                    reads=[b_TT[1]], writes=[b_TT[0]])
                P.op("act", lambda e: e.activation(
                    out=Elp[:, 0:8].rearrange("p (c o) -> p c o", o=1),
                    in_=T1buf[:, 0:1024].rearrange("p (c i) -> p c i", i=128)[:, :, 127:128],
                    func=AF.Exp, scale=-1.0 / 16), reads=[b_TT[0]], writes=[b_El])
                P.op("act", lambda e: e.activation(out=Elp[:, 8:9], in_=T1buf[:, NPR - 1:NPR], func=AF.Exp, scale=-1.0 / 16),
                     reads=[b_TT[0]], writes=[b_El])
                P.op("act", lambda e: e.activation(
                    out=Els[:, 0:16].rearrange("p (c o) -> p c o", o=1),
                    in_=T1buf[:, NPR:NT].rearrange("p (c i) -> p c i", i=8)[:, :, 7:8],
                    func=AF.Exp, scale=-1.0 / 16), reads=[b_TT[0]], writes=[b_El])
                P.op("act", lambda e: e.activation(out=T2buf, in_=T1buf, func=AF.Exp, scale=-1.0 / 16),
                     reads=[b_TT[0]], writes=[b_TT[1]])
                sQ = load_w(W, l, 0, 16, 3072 + 128 * hd, 128)
                proj_fm(psA, b_psA, sQ, 0, 16, xn_rhs, b_xn)
                for t, (t0, sz) in enumerate(TT):
                    P.op("dve", lambda e, t=t, t0=t0, sz=sz: e.scalar_tensor_tensor(
                        out=qtl[:, t0:t0 + sz], in0=psA[t][:, 0:sz], scalar=DK ** -0.5, in1=T2buf[:, t0:t0 + sz],
                        op0=ALU.mult, op1=ALU.mult),
                        reads=[b_psA[t], b_TT[1]], writes=[b_qtl])
                P.op("act", lambda e: e.activation(out=T2buf, in_=T1buf, func=AF.Exp, scale=1.0 / 16),
                     reads=[b_TT[0]], writes=[b_TT[1]])
                sK = load_w(W, l, 0, 16, 3584 + 128 * hd, 128)
                proj_fm(psB, b_psB, sK, 0, 16, xn_rhs, b_xn)
                for t, (t0, sz) in enumerate(TT):
                    P.op("dve", lambda e, t=t, t0=t0, sz=sz: e.tensor_tensor(
                        out=ktl[:, t0:t0 + sz], in0=psB[t][:, 0:sz], in1=T2buf[:, t0:t0 + sz], op=ALU.mult),
                        reads=[b_psB[t], b_TT[1]], writes=[b_ktl])
                for c, (s0, n) in enumerate(PCH):
                    P.op("dve", lambda e, c=c, s0=s0, n=n: e.tensor_scalar(
                        out=khT[:, s0:s0 + n], in0=ktl[:, s0:s0 + n], scalar1=Elp[:, c:c + 1], scalar2=None,
                        op0=ALU.mult),
                        reads=[b_ktl, b_El], writes=[b_khT])
                P.op("dve", lambda e: e.tensor_tensor(
                    out=khT[:, NPR:NT].rearrange("p (j i) -> p j i", i=8),
                    in0=ktl[:, NPR:NT].rearrange("p (j i) -> p j i", i=8),
                    in1=Els[:, 0:16].rearrange("p (j o) -> p j o", o=1).to_broadcast([128, 16, 8]),
                    op=ALU.mult),
                    reads=[b_ktl, b_El], writes=[b_khT])
                for c, (s0, n) in enumerate(ACH):
                    P.op("pe", lambda e, s0=s0, n=n: e.matmul(T2[0:n, 256:384], lhsT=khT[:, s0:s0 + n], rhs=ident_bf[:],
                                                               start=True, stop=True),
                         reads=[b_khT, b_identbf], writes=[b_T2b])
                    P.op("act", lambda e, c=c, n=n: e.activation(out=khtok[0:n, c, :], in_=T2[0:n, 256:384], func=AF.Copy),
                         reads=[b_T2b], writes=[b_khtok])
                sV = load_w(W, l, 0, 16, 4096 + 256 * hd, 256)
                for c, (s0, n) in enumerate(ACH):
                    pt = psB[c % 3]
                    bpt = b_psB[c % 3]

                    def f(e, s0=s0, n=n, pt=pt, sV=sV):
                        ins = None
                        for k in range(16):
                            ins = e.matmul(pt[0:n, 0:256], lhsT=xn[:, k, s0:s0 + n], rhs=wring[sV][:, k, 0:256],
                                           start=(k == 0), stop=(k == 15))
                        return ins
                    P.op("pe", f, reads=[b_w[sV]] + b_xn, writes=[bpt])
                    if c % 2 == 0:
                        P.op("act", lambda e, c=c, n=n, pt=pt: e.activation(out=vtok[0:n, c, :], in_=pt[0:n, 0:256], func=AF.Copy),
                             reads=[bpt], writes=[b_vtok])
                    else:
                        P.op("dve", lambda e, c=c, n=n, pt=pt: e.tensor_copy(out=vtok[0:n, c, :], in_=pt[0:n, 0:256]),
                             reads=[bpt], writes=[b_vtok])
                for c, (s0, n) in enumerate(PCH):
                    P.op("pe", lambda e, c=c, n=n: e.matmul(T1[:, 128:384], lhsT=khtok[0:n, c, :], rhs=vtok[0:n, c, :],
                                                             start=True, stop=True),
                         reads=[b_khtok, b_vtok], writes=[b_T1b])
                    if c == 0:
                        P.op("dve", lambda e: e.tensor_copy(out=pay[:, 0:256], in_=T1[:, 128:384]),
                             reads=[b_T1b], writes=[b_pay])
                    else:
                        P.op("dve", lambda e, c=c: e.scalar_tensor_tensor(
                            out=pay[:, 0:256], in0=pay[:, 0:256], scalar=Elp[:, c:c + 1], in1=T1[:, 128:384],
                            op0=ALU.mult, op1=ALU.add),
                            reads=[b_T1b, b_El], writes=[b_pay])
                b_ccs, b_ccd = Buf(), Buf()
                P.dma_rr("sp", "cx", 2, cc_src[l][hd], pay[:], reads=[b_pay], writes=[b_ccs])
                P.op("pool", lambda e, hd=hd: e.collective_compute(
                    "AllGather", ALU.bypass, replica_groups=[[0, 1], [2, 3], [4, 5], [6, 7]],
                    ins=[cc_src[l][hd]], outs=[cc_dst[l][hd]]), reads=[b_ccs], writes=[b_ccd])
                P.dma_rr("sp", "cx", 2, sin[:], cc_dst[l][hd][0:128, :], reads=[b_ccd], writes=[b_sin])
                P.op("dve", lambda e: e.tensor_scalar(out=S32, in0=sin[:, 0:256], scalar1=role[:, 0:1], scalar2=None,
                                                      op0=ALU.mult),
                     reads=[b_sin, b_role], writes=[b_S])
                P.op("act", lambda e: e.activation(out=S_bf, in_=S32, func=AF.Copy), reads=[b_S], writes=[b_Sbf])
                if hd == 0:
                    P.op("dve", lambda e: e.tensor_scalar(out=halo_in, in0=sin[:, 256:272], scalar1=role[:, 0:1],
                                                          scalar2=None, op0=ALU.mult),
                         reads=[b_sin, b_role], writes=[b_halo])
                o_sb = TTb
                for c, (s0, n) in enumerate(PCH):
                    si = c % 2
                    P.op("pe", lambda e, s0=s0, n=n: e.matmul(T1[0:n, 0:n], lhsT=ktl[:, s0:s0 + n], rhs=qtl[:, s0:s0 + n],
                                                               start=True, stop=True),
                         reads=[b_ktl, b_qtl], writes=[b_T1a])
                    P.op("dve", lambda e, n=n, si=si: e.tensor_tensor(out=sc_sb[si][0:n, 0:n], in0=T1[0:n, 0:n],
                                                                      in1=maskU[0:n, 0:n], op=ALU.mult),
                         reads=[b_T1a, b_maskU], writes=[b_sc[si]])

                    def f(e, c=c, s0=s0, n=n, si=si):
                        ins = None
                        for vc in range(2):
                            e.matmul(T2[:, vc * 128:vc * 128 + n], lhsT=S_bf[:, vc * 128:(vc + 1) * 128],
                                     rhs=qtl[:, s0:s0 + n], start=True, stop=False)
                            ins = e.matmul(T2[:, vc * 128:vc * 128 + n], lhsT=vtok[0:n, c, vc * 128:(vc + 1) * 128],
                                           rhs=sc_sb[si][0:n, 0:n], start=False, stop=True)
                        return ins
                    P.op("pe", f, reads=[b_Sbf, b_qtl, b_vtok, b_sc[si]], writes=[b_T2a])
                    P.op("act", lambda e, s0=s0, n=n: e.activation(
                        out=o_sb[:, :, s0:s0 + n], in_=T2[:, 0:256].rearrange("p (v t) -> p v t", v=2)[:, :, 0:n],
                        func=AF.Copy), reads=[b_T2a], writes=b_TT)
                    P.op("pe", lambda e, c=c, n=n: e.matmul(T1[:, 128:384], lhsT=khtok[0:n, c, :], rhs=vtok[0:n, c, :],
                                                             start=True, stop=True),
                         reads=[b_khtok, b_vtok], writes=[b_T1b])
                    P.op("dve", lambda e, c=c: e.scalar_tensor_tensor(
                        out=S32, in0=S32, scalar=Elp[:, c:c + 1], in1=T1[:, 128:384], op0=ALU.mult, op1=ALU.add),
                        reads=[b_T1b, b_El], writes=[b_S])
                    if c < len(PCH) - 1:
                        P.op("act", lambda e: e.activation(out=S_bf, in_=S32, func=AF.Copy), reads=[b_S], writes=[b_Sbf])
                note_out(P.dma_rr("sp", "st", 2, gla_p[l, hd], S32, reads=[b_S]))
                P.op("pe", lambda e: e.matmul(T1[:, 0:128], lhsT=ktl[:, NPR:NT], rhs=qtl[:, NPR:NT], start=True, stop=True),
                     reads=[b_ktl, b_qtl], writes=[b_T1a])
                P.op("dve", lambda e: e.tensor_tensor(out=sc_sb[0][:], in0=T1[:, 0:128], in1=maskBD[:], op=ALU.mult),
                     reads=[b_T1a, b_maskBD], writes=[b_sc[0]])

                def f(e):
                    ins = None
                    for vc in range(2):
                        ins = e.matmul(T2[:, vc * 128:(vc + 1) * 128], lhsT=vtok[:, 9, vc * 128:(vc + 1) * 128],
                                       rhs=sc_sb[0][:], start=True, stop=True)
                    return ins
                P.op("pe", f, reads=[b_vtok, b_sc[0]], writes=[b_T2a])
                P.op("act", lambda e: e.activation(out=o_sb[:, :, NPR:NT],
                                                   in_=T2[:, 0:256].rearrange("p (v t) -> p v t", v=2), func=AF.Copy),
                     reads=[b_T2a], writes=b_TT)
                P.op("dve", lambda e: e.tensor_tensor(
                    out=km[:], in0=khtok[:, 9, :].rearrange("p (o d) -> p o d", o=1).to_broadcast([128, 16, 128]),
                    in1=MS[:].rearrange("p (j o) -> p j o", o=1).to_broadcast([128, 16, 128]), op=ALU.mult),
                    reads=[b_khtok, b_MS], writes=[b_km])
                for j in range(16):
                    sj = j % 2
                    P.dma_rr("sp", "lds", 2, Sj[sj], sgla_d[l, j, hd], writes=[b_Sj[sj]])
                    P.op("act", lambda e, sj=sj: e.activation(out=Sj_bf[sj], in_=Sj[sj], func=AF.Copy),
                         reads=[b_Sj[sj]], writes=[b_Sjbf[sj]])

                    def f(e, j=j, sj=sj):
                        ins = None
                        for vc in range(2):
                            ins = e.matmul(T2[:, 256 + vc * 128 + 8 * j:256 + vc * 128 + 8 * j + 8],
                                           lhsT=Sj_bf[sj][:, vc * 128:(vc + 1) * 128],
                                           rhs=qtl[:, NPR + 8 * j:NPR + 8 * j + 8], start=True, stop=True)
                        return ins
                    P.op("pe", f, reads=[b_Sjbf[sj], b_qtl], writes=[b_T2b])
                    P.op("pe", lambda e, j=j: e.matmul(T1[:, 128:384], lhsT=km[:, j, :], rhs=vtok[:, 9, :],
                                                        start=True, stop=True),
                         reads=[b_km, b_vtok], writes=[b_T1b])
                    P.op("dve", lambda e, j=j, sj=sj: e.scalar_tensor_tensor(
                        out=Snew[sj], in0=Sj[sj], scalar=Els[:, j:j + 1], in1=T1[:, 128:384], op0=ALU.mult, op1=ALU.add),
                        reads=[b_T1b, b_Sj[sj], b_El], writes=[b_Snew[sj]])
                    note_out(P.dma_rr("sp", "st", 2, gla_s[l, j, hd], Snew[sj], reads=[b_Snew[sj]]))
                P.op("dve", lambda e: e.tensor_tensor(
                    out=o_sb[:, :, NPR:NT], in0=T2[:, 256:512].rearrange("p (v t) -> p v t", v=2),
                    in1=o_sb[:, :, NPR:NT], op=ALU.add),
                    reads=[b_T2b] + b_TT, writes=b_TT)
                for vc in range(2):
                    P.op("act", lambda e, vc=vc: e.activation(out=sq[vc][:], in_=o_sb[:, vc, :], func=AF.Square),
                         reads=[b_TT[vc]], writes=[b_sq[vc]])
                    ones_mm(psA, b_psA, sq[vc], b_sq[vc], vc == 0, vc == 1)
                stats_to_rstd(psA, b_psA, 1.0 / 256)
                sG = load_w(W, l, 0, 16, 5120 + 256 * hd, 256)
                for vc in range(2):
                    kk = 2 * hd + vc
                    proj_fm(psB, b_psB, sG, 128 * vc, 16, xn_rhs, b_xn)
                    for t, (t0, sz) in enumerate(TT):
                        P.op("act", lambda e, t=t, t0=t0, sz=sz, kk=kk: e.activation(
                            out=ymix8[:, kk, t0:t0 + sz], in_=psB[t][:, 0:sz], func=AF.Silu),
                            reads=[b_psB[t]], writes=[b_ymix[kk]])
                    P.op("dve", lambda e, vc=vc: e.scalar_tensor_tensor(
                        out=o_sb[:, vc, :], in0=o_sb[:, vc, :], scalar=vecs[:, vb + 84 + vc:vb + 85 + vc], in1=rstd[:],
                        op0=ALU.mult, op1=ALU.mult),
                        reads=[b_TT[vc], b_vecs, b_rstd], writes=[b_TT[vc]])
                    P.op("dve", lambda e, vc=vc, kk=kk: e.tensor_tensor(
                        out=ymix8[:, kk, :], in0=o_sb[:, vc, :], in1=ymix8[:, kk, :], op=ALU.mult),
                        reads=[b_TT[vc], b_ymix[kk]], writes=[b_ymix[kk]])

            def wout_half(row0):
                for blk in range(8):
                    sO = load_w("wout", l, row0, 8, 256 * blk, 256)
                    for i in range(2):
                        dc = 2 * blk + i
                        pset, bset = (psA, b_psA) if dc % 2 == 0 else (psB, b_psB)
                        proj_fm(pset, bset, sO, 128 * i, 8, lambda k, t0, sz: ymix8[:, k, t0:t0 + sz], b_ymix)
                        for t, (t0, sz) in enumerate(TT):
                            P.op("dve", lambda e, t=t, t0=t0, sz=sz, dc=dc, pset=pset: e.tensor_tensor(
                                out=h[:, dc, t0:t0 + sz], in0=pset[t][:, 0:sz], in1=h[:, dc, t0:t0 + sz], op=ALU.add),
                                reads=[bset[t], b_h[dc]], writes=[b_h[dc]])

            wout_half(1024)

            barrier(G_gla32 + G_conv32)
            for half in range(2):
                P.dma_rr("sp", "ldc", 1, stg[0:32, :], sconv_d[l, :, 512 * half:512 * (half + 1)], writes=[b_stg])
                for c4 in range(4):
                    cc = 4 * half + c4
                    P.op("pe", lambda e, c4=c4: e.transpose(T2[:, 256:288], stg[0:32, c4 * 128:(c4 + 1) * 128],
                                                            ident[0:32, 0:32]),
                         reads=[b_stg, b_ident], writes=[b_T2b])
                    P.op("act", lambda e, cc=cc: e.activation(out=sconvT[:, cc, :], in_=T2[:, 256:288], func=AF.Copy),
                         reads=[b_T2b], writes=[b_sconvT])
            upad_s = upad[:, 1034:1194].rearrange("p (j i) -> p j i", i=10)
            for cp in range(4):
                sB_ = load_w(W, l, 0, 16, 256 * cp, 256)
                sC_ = load_w(W, l, 0, 16, 1024 + 256 * cp, 256)
                sH_ = load_w(W, l, 0, 16, 2048 + 256 * cp, 256)
                for i in range(2):
                    cc = 2 * cp + i
                    proj_fm(psA, b_psA, sC_, 128 * i, 16, xn_rhs, b_xn)
                    for t, (t0, sz) in enumerate(TT):
                        P.op("act", lambda e, t=t, t0=t0, sz=sz: e.activation(out=cC_sb[:, t0:t0 + sz], in_=psA[t][:, 0:sz],
                                                                             func=AF.Copy),
                             reads=[b_psA[t]], writes=[b_cC])
                    proj_fm(psB, b_psB, sH_, 128 * i, 16, xn_rhs, b_xn)
                    for t, (t0, sz) in enumerate(TT[:2]):
                        P.op("dve", lambda e, t=t, t0=t0, sz=sz: e.tensor_tensor(
                            out=upad[:, 2 + t0:2 + t0 + sz], in0=psB[t][:, 0:sz], in1=cC_sb[:, t0:t0 + sz], op=ALU.mult),
                            reads=[b_psB[t], b_cC], writes=[b_upad])
                    P.op("dve", lambda e: e.tensor_tensor(
                        out=upad[:, 778:1034], in0=psB[2][:, 0:256], in1=cC_sb[:, 776:1032], op=ALU.mult),
                        reads=[b_psB[2], b_cC], writes=[b_upad])
                    P.op("dve", lambda e: e.tensor_tensor(
                        out=upad_s[:, :, 2:10], in0=psB[2][:, 256:384].rearrange("p (j i) -> p j i", i=8),
                        in1=cC_sb[:, NPR:NT].rearrange("p (j i) -> p j i", i=8), op=ALU.mult),
                        reads=[b_psB[2], b_cC], writes=[b_upad])
                    P.op("act", lambda e, cc=cc: e.activation(out=upad[:, 0:2], in_=halo_in[:, 2 * cc:2 * cc + 2], func=AF.Copy),
                         reads=[b_halo], writes=[b_upad])
                    P.op("act", lambda e, cc=cc: e.activation(
                        out=upad_s[:, :, 0:2], in_=sconvT[:, cc, :].rearrange("p (j r) -> p j r", r=2), func=AF.Copy),
                        reads=[b_sconvT], writes=[b_upad])
                    P.op("act", lambda e, cc=cc: e.activation(out=uo[:, cc, 0:2], in_=upad[:, 1032:1034], func=AF.Copy),
                         reads=[b_upad], writes=[b_uo])
                    P.op("act", lambda e, cc=cc: e.activation(
                        out=uo[:, cc, 2:34].rearrange("p (j r) -> p j r", r=2), in_=upad_s[:, :, 8:10], func=AF.Copy),
                        reads=[b_upad], writes=[b_uo])
                    proj_fm(psA, b_psA, sB_, 128 * i, 16, xn_rhs, b_xn)
                    cw = vb + 48
                    P.op("dve", lambda e, cc=cc: e.tensor_scalar(
                        out=cv[:, 0:1192], in0=upad[:, 0:1192], scalar1=vecs[:, cw + cc:cw + cc + 1], scalar2=None,
                        op0=ALU.mult), reads=[b_upad, b_vecs], writes=[b_cv])
                    for jj in (1, 2):
                        P.op("dve", lambda e, cc=cc, jj=jj: e.scalar_tensor_tensor(
                            out=cv[:, 0:1192], in0=upad[:, jj:jj + 1192],
                            scalar=vecs[:, cw + 8 * jj + cc:cw + 8 * jj + cc + 1], in1=cv[:, 0:1192],
                            op0=ALU.mult, op1=ALU.add), reads=[b_upad, b_vecs, b_cv], writes=[b_cv])
                    for t, (t0, sz) in enumerate(TT[:2]):
                        P.op("dve", lambda e, t=t, t0=t0, sz=sz: e.tensor_tensor(
                            out=zt[:, t0:t0 + sz], in0=psA[t][:, 0:sz], in1=cv[:, t0:t0 + sz], op=ALU.mult),
                            reads=[b_psA[t], b_cv], writes=[b_z])
                    P.op("dve", lambda e: e.tensor_tensor(
                        out=zt[:, 776:1032], in0=psA[2][:, 0:256], in1=cv[:, 776:1032], op=ALU.mult),
                        reads=[b_psA[2], b_cv], writes=[b_z])
                    P.op("dve", lambda e: e.tensor_tensor(
                        out=zt[:, NPR:NT].rearrange("p (j i) -> p j i", i=8),
                        in0=psA[2][:, 256:384].rearrange("p (j i) -> p j i", i=8),
                        in1=cv[:, 1034:1194].rearrange("p (j i) -> p j i", i=10)[:, :, 0:8], op=ALU.mult),
                        reads=[b_psA[2], b_cv], writes=[b_z])
                    P.op("act", lambda e: e.activation(out=sq[0][:], in_=zt, func=AF.Square), reads=[b_z], writes=[b_sq[0]])
                    ones_mm(psB, b_psB, sq[0], b_sq[0], True, True)
                    stats_to_rstd(psB, b_psB, 1.0 / 128)
                    P.op("dve", lambda e, cc=cc: e.scalar_tensor_tensor(
                        out=ymix8[:, cc, :], in0=zt, scalar=vecs[:, vb + 72 + cc:vb + 73 + cc], in1=rstd[:],
                        op0=ALU.mult, op1=ALU.mult),
                        reads=[b_z, b_vecs, b_rstd], writes=[b_ymix[cc]])
            for half in range(2):
                def f(e, half=half):
                    ins = None
                    for c4 in range(4):
                        cc = 4 * half + c4
                        ins = e.transpose(T1[0:34, c4 * 128:(c4 + 1) * 128], uo[:, cc, :], ident[:])
                    return ins
                P.op("pe", f, reads=[b_uo, b_ident], writes=[b_T1a, b_T1b])
                P.op("act", lambda e: e.activation(out=stg[0:34, :], in_=T1[0:34, :], func=AF.Copy),
                     reads=[b_T1a, b_T1b], writes=[b_stg])
                note_out(P.dma_rr("sp", "st", 2, conv_p[l, :, 512 * half:512 * (half + 1)], stg[0:2, :], reads=[b_stg]))
                note_out(P.dma_rr("sp", "st", 2, conv_s[l, :, 512 * half:512 * (half + 1)], stg[2:34, :], reads=[b_stg]))
            wout_half(0)

        for l in range(depth):
            ffn(l, 0, NVL * l + 0)
            mixer(l)
            ffn(l, 1, NVL * l + 32)

        if dbg:
            note_out(P.dma("sp", "stdbg", dbg_h, h[:], reads=b_h))

        barrier(G_all)
        h_stats()
        gf = NVL * DEPTH
        for kc in range(16):
            P.op("dve", lambda e, kc=kc: e.scalar_tensor_tensor(
                out=h[:, kc, :], in0=h[:, kc, :], scalar=vecs[:, gf + kc:gf + kc + 1], in1=rstd[:],
                op0=ALU.mult, op1=ALU.mult),
                reads=[b_h[kc], b_vecs, b_rstd], writes=[b_h[kc]])
        ytok = xtok
        b_ytok = b_xtok
        gi = 0
        for c, (c0, n) in enumerate(XCH):
            s = c % 2
            for g in range(4):
                Tt, bT = Tb[gi % 2]

                def f(e, g=g, c0=c0, n=n, Tt=Tt):
                    ins = None
                    for j in range(4):
                        kc = 4 * g + j
                        ins = e.transpose(Tt[0:n, j * 128:(j + 1) * 128], h[:, kc, c0:c0 + n], ident[:])
                    return ins
                P.op("pe", f, reads=b_h[4 * g:4 * g + 4] + [b_ident], writes=bT)
                if gi % 2 == 0:
                    P.op("dve", lambda e, s=s, g=g, n=n, Tt=Tt: e.tensor_copy(out=ytok[s][0:n, 512 * g:512 * (g + 1)],
                                                                             in_=Tt[0:n, :]),
                         reads=bT, writes=[b_ytok[s]])
                else:
                    P.op("act", lambda e, s=s, g=g, n=n, Tt=Tt: e.activation(out=ytok[s][0:n, 512 * g:512 * (g + 1)],
                                                                            in_=Tt[0:n, :], func=AF.Copy),
                         reads=bT, writes=[b_ytok[s]])
                gi += 1
            note_out(P.dma_rr("sp", "sty", 2, y_out[c0:c0 + n, :], ytok[s][0:n, :], reads=[b_ytok[s]]))

        P.wait_all("sp", list(out_toks.items()))
        P.build()
    return nc


def _cols(v):
    v = np.asarray(v, np.float32)
    return np.ascontiguousarray(v.reshape(-1, 128).T)


_NC_CACHE = {}


def make_in_maps(x_prompt, x_sample, state_conv, state_gla, meta_tokens, norm_ffn1, w_ffn1_gu,
           w_ffn1_down, norm_mix, w_mix_in, conv_w, conv_norm, gla_fgate_w2, gla_fgate_b,
           gla_out_norm, w_mix_out, norm_ffn2, w_ffn2_gu, w_ffn2_down, norm_final, depth=DEPTH):
    f32 = np.float32
    x_prompt = np.asarray(x_prompt, f32)
    x_sample = np.asarray(x_sample, f32)
    state_conv = np.asarray(state_conv, f32)
    state_gla = np.asarray(state_gla, f32)
    meta_tokens = np.asarray(meta_tokens, f32)
    ws = {
        "wgu1": np.asarray(w_ffn1_gu, f32), "wdn1": np.asarray(w_ffn1_down, f32),
        "win": np.asarray(w_mix_in, f32), "wout": np.asarray(w_mix_out, f32),
        "wgu2": np.asarray(w_ffn2_gu, f32), "wdn2": np.asarray(w_ffn2_down, f32),
    }
    vecs = np.zeros((128, NV), f32)
    for l in range(DEPTH):
        b = NVL * l
        vecs[:, b + 0:b + 16] = _cols(norm_ffn1[l])
        vecs[:, b + 16:b + 32] = _cols(norm_mix[l])
        vecs[:, b + 32:b + 48] = _cols(norm_ffn2[l])
        for j in range(3):
            vecs[:, b + 48 + 8 * j:b + 56 + 8 * j] = _cols(np.asarray(conv_w)[l, j])
        vecs[:, b + 72:b + 80] = _cols(conv_norm[l])
        vecs[:, b + 80:b + 84] = _cols(gla_fgate_b[l])
        vecs[:, b + 84:b + 86] = _cols(gla_out_norm[l])
    vecs[:, NVL * DEPTH:NVL * DEPTH + 16] = _cols(norm_final)
    fw2 = np.ascontiguousarray(np.asarray(gla_fgate_w2, f32))

    in_maps = []
    for c in range(8):
        p, r = c // 2, c % 2
        if r == 0:
            xp = np.concatenate([meta_tokens, x_prompt[p, 0:1016]], axis=0)
        else:
            xp = x_prompt[p, 1016:2048]
        xs = x_sample[16 * c:16 * (c + 1)].reshape(128, D)
        role = np.zeros((128, 8), f32)
        role[:, 0] = float(r)
        if r == 1:
            role[:, 1 + p] = 1.0
        m = {
            "x_in": np.ascontiguousarray(np.concatenate([xp, xs], axis=0)),
            "vecs": vecs, "role": role, "fw2": fw2,
            "sconv": np.ascontiguousarray(state_conv[:, 16 * c:16 * (c + 1)].reshape(DEPTH, 32, 1024)),
            "sgla": np.ascontiguousarray(state_gla[:, 16 * c:16 * (c + 1)]),
        }
        for name, w in ws.items():
            m[name] = w[:depth]
        in_maps.append(m)

    return in_maps


def kernel(x_prompt, x_sample, state_conv, state_gla, meta_tokens, norm_ffn1, w_ffn1_gu,
           w_ffn1_down, norm_mix, w_mix_in, conv_w, conv_norm, gla_fgate_w2, gla_fgate_b,
           gla_out_norm, w_mix_out, norm_ffn2, w_ffn2_gu, w_ffn2_down, norm_final):
    f32 = np.float32
    in_maps = make_in_maps(x_prompt, x_sample, state_conv, state_gla, meta_tokens, norm_ffn1, w_ffn1_gu,
                           w_ffn1_down, norm_mix, w_mix_in, conv_w, conv_norm, gla_fgate_w2, gla_fgate_b,
                           gla_out_norm, w_mix_out, norm_ffn2, w_ffn2_gu, w_ffn2_down, norm_final)
    if "nc" not in _NC_CACHE:
        _NC_CACHE["nc"] = build_program()
    nc = _NC_CACHE["nc"]
    res = run_bass_kernel_spmd(nc, in_maps, core_ids=list(range(8)))
    R = res.results

    y_prompt = np.empty((4, 2048, D), f32)
    y_sample = np.empty((128, 8, D), f32)
    new_gla_prompt = np.empty((DEPTH, 4, 4, 128, 256), f32)
    new_conv_prompt = np.empty((DEPTH, 4, 2, 1024), f32)
    new_gla_sample = np.empty((DEPTH, 128, 4, 128, 256), f32)
    new_conv_sample = np.empty((DEPTH, 128, 2, 1024), f32)
    for c in range(8):
        p, r = c // 2, c % 2
        y = R[c]["y_out"]
        if r == 0:
            y_prompt[p, 0:1016] = y[16:1032]
        else:
            y_prompt[p, 1016:2048] = y[0:1032]
            new_gla_prompt[:, p] = R[c]["gla_p"]
            new_conv_prompt[:, p] = R[c]["conv_p"]
        y_sample[16 * c:16 * (c + 1)] = y[1032:1160].reshape(16, 8, D)
        new_gla_sample[:, 16 * c:16 * (c + 1)] = R[c]["gla_s"]
        new_conv_sample[:, 16 * c:16 * (c + 1)] = R[c]["conv_s"].reshape(DEPTH, 16, 2, 1024)
    return (y_prompt, y_sample, new_gla_prompt, new_conv_prompt, new_gla_sample, new_conv_sample)
```

```python
import contextlib
import numpy as np
import concourse.bass as bass
import concourse.mybir as mybir
from concourse.bass_utils import run_bass_kernel_spmd

F32 = mybir.dt.float32
BF16 = mybir.dt.bfloat16
AF = mybir.ActivationFunctionType
ALU = mybir.AluOpType

D = 2048
DFF = 5632
DEPTH = 4
NPR = 1032
NS = 128
NT = NPR + NS
TT = [(0, 392), (392, 384), (776, 384)]
PCH = [(i * 128, 128) for i in range(8)] + [(1024, 8)]
ACH = PCH + [(NPR, 128)]
XCH = [(i * 128, 128) for i in range(9)] + [(1152, 8)]
EPS = 1e-6
DK = 128
NVL = 86
NV = NVL * DEPTH + 16
NSLOT = 3
WSPEC = [("wgu1", D, 2 * DFF), ("wdn1", DFF, D), ("win", D, 6160), ("wout", D, D), ("wgu2", D, 2 * DFF), ("wdn2", DFF, D)]
ENGS = ("pe", "act", "dve", "pool", "sp")


class Buf:
    __slots__ = ("name", "w", "r")

    def __init__(self, name=""):
        self.name = name
        self.w = None
        self.r = []


class Prog:
    def __init__(self, nc):
        self.nc = nc
        self.q = {e: [] for e in ENGS}
        self.cnt = {e: 0 for e in ENGS}
        self.waited = {e: {} for e in ENGS}
        self.dma_sems = {}
        self.rr = {}

    def _waits_for(self, reads, writes):
        toks = []
        for b in reads:
            if b.w is not None:
                toks.append(b.w)
        for b in writes:
            if b.w is not None:
                toks.append(b.w)
            toks.extend(b.r)
        return toks

    def _filter(self, eng, toks):
        out = {}
        wd = self.waited[eng]
        for (s, v) in toks:
            if wd.get(s, 0) >= v:
                continue
            if out.get(s, 0) < v:
                out[s] = v
        for s, v in out.items():
            wd[s] = v
        return list(out.items())

    def _register(self, tok, reads, writes):
        for b in reads:
            b.r.append(tok)
        for b in writes:
            b.w = tok
            b.r = []

    def op(self, eng, fn, reads=(), writes=(), sem=None):
        toks = self._waits_for(reads, writes)
        waits = self._filter(eng, toks)
        sname = sem if sem is not None else eng
        if sem is not None:
            self.dma_sems.setdefault(sem, 0)
        self.cnt[sname] = self.cnt.get(sname, 0) + 1
        tok = (sname, self.cnt[sname])
        self.q[eng].append((waits, fn, (sname, 1)))
        self._register(tok, reads, writes)
        return tok

    def dma(self, queue, sem, out_ap, in_ap, reads=(), writes=()):
        n = self.dma_sems.get(sem, 0)
        toks = self._waits_for(reads, writes)
        if n > 0:
            toks.append((sem, 16 * n))
        waits = self._filter(queue, toks)
        self.dma_sems[sem] = n + 1
        tok = (sem, 16 * (n + 1))

        def fn(e, out_ap=out_ap, in_ap=in_ap):
            return e.dma_start(out=out_ap, in_=in_ap)
        self.q[queue].append((waits, fn, (sem, 16)))
        self._register(tok, reads, writes)
        return tok

    def dma_rr(self, queue, pool, n, out_ap, in_ap, reads=(), writes=()):
        i = self.rr.get(pool, 0)
        self.rr[pool] = i + 1
        return self.dma(queue, "%s%d" % (pool, i % n), out_ap, in_ap, reads, writes)

    def wait_all(self, eng, toks):
        waits = self._filter(eng, toks)
        self.q[eng].append((waits, None, None))

    def build(self):
        nc = self.nc
        names = list(ENGS) + sorted(self.dma_sems.keys())
        with contextlib.ExitStack() as st:
            sems = {}
            for n in names:
                sems[n] = st.enter_context(nc.semaphore("s_" + n))
            block = st.enter_context(nc.Block())

            def run(e, ename):
                for waits, fn, inc in self.q[ename]:
                    for (s, v) in waits:
                        e.wait_ge(sems[s], v)
                    if fn is not None:
                        ins = fn(e)
                        ins.then_inc(sems[inc[0]], inc[1])

            @block.tensor
            def _(e):
                run(e, "pe")

            @block.scalar
            def _(e):
                run(e, "act")

            @block.vector
            def _(e):
                run(e, "dve")

            @block.gpsimd
            def _(e):
                run(e, "pool")

            @block.sync
            def _(e):
                run(e, "sp")


def build_program(depth=DEPTH, dbg=False):
    nc = bass.Bass("TRN2", target_bir_lowering=False)

    def din(name, shape):
        return nc.dram_tensor(name, list(shape), F32, kind="ExternalInput").ap()

    def dout(name, shape):
        return nc.dram_tensor(name, list(shape), F32, kind="ExternalOutput").ap()

    x_in = din("x_in", [NT, D])
    vecs_d = din("vecs", [128, NV])
    role_d = din("role", [128, 8])
    fw2_d = din("fw2", [DEPTH, 16, 512])
    sconv_d = din("sconv", [DEPTH, 32, 1024])
    sgla_d = din("sgla", [DEPTH, 16, 4, 128, 256])
    w_ext, w_full, b_wfull = {}, {}, {}
    for name, Rr, Cc in WSPEC:
        w_ext[name] = din(name, [depth, Rr, Cc])
        w_full[name] = [w_ext[name][l] for l in range(depth)]
        b_wfull[name] = [Buf() for l in range(depth)]
    wgu = ["wgu1", "wgu2"]
    wdn = ["wdn1", "wdn2"]

    y_out = dout("y_out", [NT, D])
    gla_p = dout("gla_p", [DEPTH, 4, 128, 256])
    conv_p = dout("conv_p", [DEPTH, 2, 1024])
    gla_s = dout("gla_s", [DEPTH, 16, 4, 128, 256])
    conv_s = dout("conv_s", [DEPTH, 32, 1024])
    if dbg:
        dbg_h = dout("dbg_h", [128, 16, NT])

    cc_src = [[nc.dram_tensor("cc_src_%d_%d" % (l, hd), [128, 272], F32).ap() for hd in range(4)] for l in range(depth)]
    cc_dst = [[nc.dram_tensor("cc_dst_%d_%d" % (l, hd), [256, 272], F32).ap() for hd in range(4)] for l in range(depth)]

    es = contextlib.ExitStack()
    with es:
        def sb(name, shape, dt):
            return es.enter_context(nc.sbuf_tensor(name, list(shape), dt))

        def ps(name, shape, dt=F32):
            return es.enter_context(nc.psum_tensor(name, list(shape), dt))

        h = sb("h", [128, 16, NT], F32)
        xn = sb("xn", [128, 16, NT], BF16)
        vecs = sb("vecs_sb", [128, NV], F32)
        negfb = sb("negfb", [128, 16], F32)
        role = sb("role_sb", [128, 8], F32)
        cst = sb("cst", [128, 4], F32)
        ident = sb("ident", [128, 128], F32)
        ident_bf = sb("ident_bf", [128, 128], BF16)
        ones_bf = sb("ones_bf", [128, 128], BF16)
        ones_f = sb("ones_f", [128, 2], F32)
        maskU = sb("maskU", [128, 128], F32)
        maskBD = sb("maskBD", [128, 128], BF16)
        MS = sb("MS", [128, 16], BF16)
        MSf = sb("MSf", [128, 16], F32)
        MSTf = sb("MSTf", [16, 128], F32)
        MST = sb("MST", [16, 128], BF16)
        fw2 = sb("fw2_sb", [16, 512], F32)
        rstd = sb("rstd", [128, NT], F32)
        sq0 = sb("sq0", [128, NT], BF16)
        sq = [sq0, sq0]
        wring = [sb("wring%d" % i, [128, 16, 256], BF16) for i in range(NSLOT)]
        N16 = 18512
        N32 = 5400
        scr16 = sb("scr16", [128, N16], BF16)
        scr32 = sb("scr32", [128, N32], F32)

        psA = [ps("psA%d" % i, [128, 512]) for i in range(3)]
        psB = [ps("psB%d" % i, [128, 512]) for i in range(3)]
        T1 = ps("T1", [128, 512])
        T2 = ps("T2", [128, 512])

        P = Prog(nc)

        b_h = [Buf("h%d" % k) for k in range(16)]
        b_xn = [Buf("xn%d" % k) for k in range(16)]
        b_vecs, b_negfb, b_role, b_cst = Buf(), Buf(), Buf(), Buf()
        b_ident, b_identbf, b_ones, b_onesf = Buf(), Buf(), Buf(), Buf()
        b_maskU, b_maskBD, b_MS, b_MSf, b_MSTf, b_MST, b_fw2 = Buf(), Buf(), Buf(), Buf(), Buf(), Buf(), Buf()
        b_rstd = Buf()
        b_sq0 = Buf()
        b_sq = [b_sq0, b_sq0]
        b_w = [Buf("w%d" % i) for i in range(NSLOT)]
        b_psA = [Buf() for _ in range(3)]
        b_psB = [Buf() for _ in range(3)]
        b_T1a, b_T1b, b_T2a, b_T2b = Buf(), Buf(), Buf(), Buf()

        out_toks = {}

        def note_out(tok):
            out_toks[tok[0]] = max(out_toks.get(tok[0], 0), tok[1])

        P.op("pool", lambda e: e.memset(ident[:], 0.0), writes=[b_ident])
        P.op("pool", lambda e: e.affine_select(out=ident[:], in_=ident[:], pattern=[[-1, 128]],
                                               compare_op=ALU.not_equal, fill=1.0, base=0, channel_multiplier=1),
             reads=[b_ident], writes=[b_ident])
        P.op("pool", lambda e: e.memset(ones_bf[:], 1.0), writes=[b_ones])
        P.op("pool", lambda e: e.memset(ones_f[:], 1.0), writes=[b_onesf])
        P.op("pool", lambda e: e.memset(cst[:, 0:1], EPS), writes=[b_cst])
        P.op("pool", lambda e: e.memset(cst[:, 1:2], 1.0), reads=[b_cst], writes=[b_cst])
        P.op("pool", lambda e: e.memset(maskU[:], 1.0), writes=[b_maskU])
        P.op("pool", lambda e: e.affine_select(out=maskU[:], in_=maskU[:], pattern=[[1, 128]],
                                               compare_op=ALU.is_ge, fill=0.0, base=0, channel_multiplier=-1),
             reads=[b_maskU], writes=[b_maskU])
        P.op("pool", lambda e: e.memset(MSf[:], 1.0), writes=[b_MSf])
        P.op("pool", lambda e: e.affine_select(out=MSf[:], in_=MSf[:], pattern=[[-8, 16]],
                                               compare_op=ALU.is_ge, fill=0.0, base=0, channel_multiplier=1),
             reads=[b_MSf], writes=[b_MSf])
        P.op("pool", lambda e: e.affine_select(out=MSf[:], in_=MSf[:], pattern=[[8, 16]],
                                               compare_op=ALU.is_ge, fill=0.0, base=7, channel_multiplier=-1),
             reads=[b_MSf], writes=[b_MSf])
        P.op("pool", lambda e: e.memset(MSTf[:], 1.0), writes=[b_MSTf])
        P.op("pool", lambda e: e.affine_select(out=MSTf[:], in_=MSTf[:], pattern=[[1, 128]],
                                               compare_op=ALU.is_ge, fill=0.0, base=0, channel_multiplier=-8),
             reads=[b_MSTf], writes=[b_MSTf])
        P.op("pool", lambda e: e.affine_select(out=MSTf[:], in_=MSTf[:], pattern=[[-1, 128]],
                                               compare_op=ALU.is_ge, fill=0.0, base=7, channel_multiplier=8),
             reads=[b_MSTf], writes=[b_MSTf])
        P.op("dve", lambda e: e.tensor_copy(out=ident_bf[:], in_=ident[:]), reads=[b_ident], writes=[b_identbf])
        P.op("dve", lambda e: e.tensor_copy(out=MS[:], in_=MSf[:]), reads=[b_MSf], writes=[b_MS])
        P.op("dve", lambda e: e.tensor_copy(out=MST[:], in_=MSTf[:]), reads=[b_MSTf], writes=[b_MST])
        P.op("pe", lambda e: e.matmul(T1[:, 0:128], lhsT=MST[0:16, :], rhs=MST[0:16, :], start=True, stop=True),
             reads=[b_MST], writes=[b_T1a])
        P.op("dve", lambda e: e.tensor_tensor(out=maskBD[:], in0=T1[:, 0:128], in1=maskU[:], op=ALU.mult),
             reads=[b_T1a, b_maskU], writes=[b_maskBD])

        P.dma("sp", "ldm", vecs[:], vecs_d, writes=[b_vecs])
        P.dma("sp", "ldm", role[:], role_d, writes=[b_role])
        for l in range(DEPTH):
            P.op("dve", lambda e, l=l: e.tensor_scalar(out=negfb[:, 4 * l:4 * l + 4],
                                                      in0=vecs[:, NVL * l + 80:NVL * l + 84],
                                                      scalar1=-1.0, scalar2=None, op0=ALU.mult),
                 reads=[b_vecs], writes=[b_negfb])

        ring_i = [0]

        def load_w(name, l, r0, nk, c0, ncols):
            s = ring_i[0] % NSLOT
            ring_i[0] += 1
            src = w_full[name][l][r0:r0 + nk * 128, c0:c0 + ncols]
            P.dma("pool", "dw%d" % s, wring[s][:, 0:nk, 0:ncols],
                  src.rearrange("(k p) m -> p k m", p=128), reads=[b_wfull[name][l]], writes=[b_w[s]])
            return s

        def gather(names, l):
            return

        def proj_fm(pset, b_pset, slot, col, nk, rhs_fn, rhs_bufs, m=128):
            def f(e):
                ins = None
                for k in range(nk):
                    for t, (t0, sz) in enumerate(TT):
                        ins = e.matmul(pset[t][0:m, 0:sz], lhsT=wring[slot][:, k, col:col + m],
                                       rhs=rhs_fn(k, t0, sz), start=(k == 0), stop=(k == nk - 1))
                return ins
            return P.op("pe", f, reads=[b_w[slot]] + list(rhs_bufs), writes=list(b_pset))

        def stats_to_rstd(pset, b_pset, inv_n):
            for t, (t0, sz) in enumerate(TT):
                P.op("act", lambda e, t=t, t0=t0, sz=sz: e.activation(
                    out=rstd[:, t0:t0 + sz], in_=pset[t][:, 0:sz], func=AF.Ln, scale=inv_n, bias=cst[:, 0:1]),
                    reads=[b_pset[t], b_cst], writes=[b_rstd])
            P.op("act", lambda e: e.activation(out=rstd[:], in_=rstd[:], func=AF.Exp, scale=-0.5),
                 reads=[b_rstd], writes=[b_rstd])

        def ones_mm(pset, b_pset, src, b_src, first, last):
            def f(e):
                ins = None
                for t, (t0, sz) in enumerate(TT):
                    ins = e.matmul(pset[t][:, 0:sz], lhsT=ones_bf[:], rhs=src[:, t0:t0 + sz], start=first, stop=last)
                return ins
            return P.op("pe", f, reads=[b_src, b_ones], writes=list(b_pset))

        def h_stats():
            for kc in range(16):
                i = kc % 2
                P.op("act", lambda e, kc=kc, i=i: e.activation(out=sq[i][:], in_=h[:, kc, :], func=AF.Square),
                     reads=[b_h[kc]], writes=[b_sq[i]])
                ones_mm(psA, b_psA, sq[i], b_sq[i], kc == 0, kc == 15)
            stats_to_rstd(psA, b_psA, 1.0 / D)

        def rmsnorm_xn(gcol0):
            h_stats()
            for kc in range(16):
                P.op("dve", lambda e, kc=kc: e.scalar_tensor_tensor(
                    out=xn[:, kc, :], in0=h[:, kc, :], scalar=vecs[:, gcol0 + kc:gcol0 + kc + 1], in1=rstd[:],
                    op0=ALU.mult, op1=ALU.mult),
                    reads=[b_h[kc], b_vecs, b_rstd], writes=[b_xn[kc]])

        def xn_rhs(k, t0, sz):
            return xn[:, k, t0:t0 + sz]

        gather(["wgu1", "wdn1", "win", "wout"], 0)

        xtok = [scr32[:, 0:2048], scr32[:, 2048:4096]]
        b_xtok = [Buf(), Buf()]
        Tb = [(T1, [b_T1a, b_T1b]), (T2, [b_T2a, b_T2b])]
        gi = 0
        for c, (c0, n) in enumerate(XCH):
            s = c % 2
            P.dma("sp", "ldx%d" % s, xtok[s][0:n, :], x_in[c0:c0 + n, :], writes=[b_xtok[s]])
            for g in range(4):
                Tt, bT = Tb[gi % 2]

                def f(e, s=s, g=g, n=n, Tt=Tt):
                    ins = None
                    for j in range(4):
                        kc = 4 * g + j
                        ins = e.transpose(Tt[:, j * 128:j * 128 + n], xtok[s][0:n, kc * 128:(kc + 1) * 128],
                                          ident[0:n, 0:n])
                    return ins
                P.op("pe", f, reads=[b_xtok[s], b_ident], writes=bT)
                src = Tt[:].rearrange("p (j m) -> p j m", j=4)[:, :, 0:n]
                dst = h[:, 4 * g:4 * g + 4, c0:c0 + n]
                if gi % 2 == 0:
                    P.op("dve", lambda e, src=src, dst=dst: e.tensor_copy(out=dst, in_=src),
                         reads=bT, writes=b_h[4 * g:4 * g + 4])
                else:
                    P.op("act", lambda e, src=src, dst=dst: e.activation(out=dst, in_=src, func=AF.Copy),
                         reads=bT, writes=b_h[4 * g:4 * g + 4])
                gi += 1

        hidden = scr16[:, 0:11 * NT].rearrange("p (j t) -> p j t", j=11)
        b_hid = [Buf() for _ in range(11)]
        sa = [scr32[:, 0:NT], scr32[:, NT:2 * NT]]
        b_sa = [Buf(), Buf()]

        def ffn(l, which, gcol0):
            W1 = wgu[which]
            W2 = wdn[which]
            barrier(G_all)
            if which == 0:
                gather(["wgu2", "wdn2"], l)
            elif l + 1 < depth:
                gather(["wgu1", "wdn1", "win", "wout"], l + 1)
            rmsnorm_xn(gcol0)
            jj = 0
            for q in range(4):
                for s6 in range(6):
                    nch = 2 if s6 < 5 else 1
                    c0 = 1408 * q + 256 * s6
                    sA = load_w(W1, l, 0, 16, c0, 128 * nch)
                    sB = load_w(W1, l, 0, 16, DFF + c0, 128 * nch)
                    for i in range(nch):
                        j = 2 * s6 + i
                        si = jj % 2
                        jj += 1
                        proj_fm(psA, b_psA, sA, 128 * i, 16, xn_rhs, b_xn)
                        for t, (t0, sz) in enumerate(TT):
                            P.op("act", lambda e, t=t, t0=t0, sz=sz, si=si: e.activation(
                                out=sa[si][:, t0:t0 + sz], in_=psA[t][:, 0:sz], func=AF.Silu),
                                reads=[b_psA[t]], writes=[b_sa[si]])
                        proj_fm(psB, b_psB, sB, 128 * i, 16, xn_rhs, b_xn)
                        for t, (t0, sz) in enumerate(TT):
                            P.op("dve", lambda e, t=t, t0=t0, sz=sz, si=si, j=j: e.tensor_tensor(
                                out=hidden[:, j, t0:t0 + sz], in0=psB[t][:, 0:sz], in1=sa[si][:, t0:t0 + sz],
                                op=ALU.mult),
                                reads=[b_psB[t], b_sa[si]], writes=[b_hid[j]])
                for blk in range(8):
                    sD = load_w(W2, l, 1408 * q, 11, 256 * blk, 256)
                    for i in range(2):
                        dc = 2 * blk + i
                        pset, bset = (psA, b_psA) if dc % 2 == 0 else (psB, b_psB)
                        proj_fm(pset, bset, sD, 128 * i, 11, lambda k, t0, sz: hidden[:, k, t0:t0 + sz], b_hid)
                        for t, (t0, sz) in enumerate(TT):
                            P.op("dve", lambda e, t=t, t0=t0, sz=sz, dc=dc, pset=pset: e.scalar_tensor_tensor(
                                out=h[:, dc, t0:t0 + sz], in0=pset[t][:, 0:sz], scalar=0.5, in1=h[:, dc, t0:t0 + sz],
                                op0=ALU.mult, op1=ALU.add),
                                reads=[bset[t], b_h[dc]], writes=[b_h[dc]])

        o16 = [0]
        o32 = [0]

        def c16(n):
            a = o16[0]
            o16[0] += n
            assert o16[0] <= N16, o16[0]
            return scr16[:, a:a + n]

        def c32(n):
            a = o32[0]
            o32[0] += n
            assert o32[0] <= N32, o32[0]
            return scr32[:, a:a + n]

        ymix8 = c16(8 * NT).rearrange("p (k t) -> p k t", k=8)
        b_ymix = [Buf() for _ in range(8)]
        qtl, ktl = c16(NT), c16(NT)
        khT = sq0
        b_qtl, b_ktl, b_khT = Buf(), Buf(), b_sq0
        khtok = c16(10 * 128).rearrange("p (c d) -> p c d", c=10)
        b_khtok = Buf()
        vtok = c16(10 * 256).rearrange("p (c d) -> p c d", c=10)
        b_vtok = Buf()
        S_bf = c16(256)
        b_Sbf = Buf()
        sc_sb = [c16(128), c16(128)]
        b_sc = [Buf(), Buf()]
        km = c16(16 * 128).rearrange("p (j d) -> p j d", j=16)
        b_km = Buf()
        Sj_bf = [c16(256), c16(256)]
        b_Sjbf = [Buf(), Buf()]
        mix16_end = o16[0]

        TTb = c32(2 * NT).rearrange("p (v t) -> p v t", v=2)
        b_TT = [Buf(), Buf()]
        S32 = c32(256)
        b_S = Buf()
        pay = c32(272)
        b_pay = Buf()
        sin = c32(272)
        b_sin = Buf()
        Sj = [c32(256), c32(256)]
        b_Sj = [Buf(), Buf()]
        Snew = [c32(256), c32(256)]
        b_Snew = [Buf(), Buf()]
        fl = c32(NT)
        b_fl = Buf()
        Elp = c32(16)
        Els = c32(16)
        b_El = Buf()
        halo_in = c32(16)
        b_halo = Buf()
        ul = c32(32).rearrange("p (g c r) -> p g c r", g=2, c=8)
        b_ul = Buf()
        gla32_end = o32[0]
        o32[0] = 0
        cC_sb = c32(NT)
        upad = c32(1194)
        cv = c32(1194)
        zt = c32(NT)
        b_cC, b_upad, b_cv, b_z = Buf(), Buf(), Buf(), Buf()
        sconvT = c32(256).rearrange("p (c r) -> p c r", c=8)
        b_sconvT = Buf()
        uo = c32(8 * 34).rearrange("p (c r) -> p c r", c=8)
        b_uo = Buf()
        conv32_end = o32[0]
        assert conv32_end <= 5304
        o32[0] = max(gla32_end, conv32_end)
        stg = sb("stg", [34, 512], F32)
        b_stg = Buf()
        G_gla32 = [b_TT[0], b_TT[1], b_S, b_pay, b_sin, b_Sj[0], b_Sj[1], b_Snew[0], b_Snew[1], b_fl]
        G_conv32 = [b_cC, b_upad, b_cv, b_z, b_sconvT, b_uo]
        G_mix16 = b_ymix + [b_qtl, b_ktl, b_khtok, b_vtok, b_Sbf, b_sc[0], b_sc[1], b_km, b_Sjbf[0], b_Sjbf[1]]
        G_ffn = b_hid + b_sa
        G_x = b_xtok
        G_all = G_gla32 + G_conv32 + G_mix16 + G_ffn + G_x + [b_El, b_halo, b_ul]

        def barrier(bufs):
            P.op("act", lambda e: e.activation(out=cst[:, 2:3], in_=cst[:, 1:2], func=AF.Copy),
                 reads=[b_cst], writes=list(bufs))

        def mixer(l):
            vb = NVL * l
            barrier(G_all)
            rmsnorm_xn(vb + 16)
            W = "win"
            P.dma("sp", "ldm", fw2[:], fw2_d[l], writes=[b_fw2])

            sF = load_w(W, l, 0, 16, 6144, 16)
            proj_fm(psA, b_psA, sF, 0, 16, xn_rhs, b_xn, m=16)
            for t, (t0, sz) in enumerate(TT):
                P.op("act", lambda e, t=t, t0=t0, sz=sz: e.activation(out=fl[0:16, t0:t0 + sz], in_=psA[t][0:16, 0:sz],
                                                                     func=AF.Copy),
                     reads=[b_psA[t]], writes=[b_fl])
            for g in range(2):
                for cp in range(4):
                    sU = load_w(W, l, 0, 16, 1024 * (g + 1) + 256 * cp, 256)
                    for i in range(2):
                        cc = 2 * cp + i

                        def f(e, sU=sU, i=i, cc=cc):
                            ins = None
                            for k in range(16):
                                ins = e.matmul(T2[:, 256 + 2 * cc:258 + 2 * cc], lhsT=wring[sU][:, k, 128 * i:128 * (i + 1)],
                                               rhs=xn[:, k, NPR - 2:NPR], start=(k == 0), stop=(k == 15))
                            return ins
                        P.op("pe", f, reads=[b_w[sU]] + b_xn, writes=[b_T2b])
                P.op("act", lambda e, g=g: e.activation(out=ul[:, g, :, :],
                                                        in_=T2[:, 256:272].rearrange("p (c r) -> p c r", c=8), func=AF.Copy),
                     reads=[b_T2b], writes=[b_ul])
            P.op("dve", lambda e: e.tensor_tensor(out=pay[:, 256:272].rearrange("p (c r) -> p c r", c=8),
                                                  in0=ul[:, 0, :, :], in1=ul[:, 1, :, :], op=ALU.mult),
                 reads=[b_ul], writes=[b_pay])

            for hd in range(4):
                T1buf = TTb[:, 0, :]
                o_sb = TTb
                T2buf = TTb[:, 1, :]
                def f(e, hd=hd):
                    ins = None
                    for t, (t0, sz) in enumerate(TT):
                        ins = e.matmul(psA[t][:, 0:sz], lhsT=fw2[0:16, hd * 128:(hd + 1) * 128],
                                       rhs=fl[0:16, t0:t0 + sz], start=True, stop=True)
                    return ins
                P.op("pe", f, reads=[b_fl, b_fw2], writes=b_psA)
                for t, (t0, sz) in enumerate(TT):
                    P.op("act", lambda e, t=t, t0=t0, sz=sz, hd=hd: e.activation(
                        out=T1buf[:, t0:t0 + sz], in_=psA[t][:, 0:sz], func=AF.Exp, scale=-1.0,
                        bias=negfb[:, 4 * l + hd:4 * l + hd + 1]),
                        reads=[b_psA[t], b_negfb], writes=[b_TT[0]])
                P.op("act", lambda e: e.activation(out=T1buf, in_=T1buf, func=AF.Ln, bias=cst[:, 1:2]),
                     reads=[b_TT[0], b_cst], writes=[b_TT[0]])
                P.op("dve", lambda e: e.tensor_tensor_scan(out=T2buf, data0=T1buf, data1=T1buf, initial=0.0,
                                                           op0=ALU.add, op1=ALU.max),
                     reads=[b_TT[0]], writes=[b_TT[1]])
                P.op("dve", lambda e: e.tensor_copy(out=T1buf[:, 0:128], in_=T2buf[:, 0:128]),
                     reads=[b_TT[1]], writes=[b_TT[0]])
                for (s0, n) in PCH[1:]:
                    P.op("dve", lambda e, s0=s0, n=n: e.tensor_scalar(
                        out=T1buf[:, s0:s0 + n], in0=T2buf[:, s0:s0 + n], scalar1=T2buf[:, s0 - 1:s0], scalar2=None,
                        op0=ALU.subtract),
                        reads=[b_TT[1]], writes=[b_TT[0]])
                P.op("dve", lambda e: e.tensor_tensor(
                    out=T1buf[:, NPR:NT].rearrange("p (j i) -> p j i", i=8),
                    in0=T2buf[:, NPR:NT].rearrange("p (j i) -> p j i", i=8),
                    in1=T2buf[:, NPR - 1:NT - 1].rearrange("p (j i) -> p j i", i=8)[:, :, 0:1].to_broadcast([128, 16, 8]),
                    op=ALU.subtract),
                    reads=[b_TT[1]], writes=[b_TT[0]])
                P.op("act", lambda e: e.activation(
                    out=Elp[:, 0:8].rearrange("p (c o) -> p c o", o=1),
                    in_=T1buf[:, 0:1024].rearrange("p (c i) -> p c i", i=128)[:, :, 127:128],
                    func=AF.Exp, scale=-1.0 / 16), reads=[b_TT[0]], writes=[b_El])
                P.op("act", lambda e: e.activation(out=Elp[:, 8:9], in_=T1buf[:, NPR - 1:NPR], func=AF.Exp, scale=-1.0 / 16),
                     reads=[b_TT[0]], writes=[b_El])
                P.op("act", lambda e: e.activation(
                    out=Els[:, 0:16].rearrange("p (c o) -> p c o", o=1),
                    in_=T1buf[:, NPR:NT].rearrange("p (c i) -> p c i", i=8)[:, :, 7:8],
                    func=AF.Exp, scale=-1.0 / 16), reads=[b_TT[0]], writes=[b_El])
                P.op("act", lambda e: e.activation(out=T2buf, in_=T1buf, func=AF.Exp, scale=-1.0 / 16),
                     reads=[b_TT[0]], writes=[b_TT[1]])
                sV = load_w(W, l, 0, 16, 4096 + 256 * hd, 256)
                for c, (s0, n) in enumerate(ACH):
                    pt = psB[c % 3]
                    bpt = b_psB[c % 3]

                    def f(e, s0=s0, n=n, pt=pt, sV=sV):
                        ins = None
                        for k in range(16):
                            ins = e.matmul(pt[0:n, 0:256], lhsT=xn[:, k, s0:s0 + n], rhs=wring[sV][:, k, 0:256],
                                           start=(k == 0), stop=(k == 15))
                        return ins
                    P.op("pe", f, reads=[b_w[sV]] + b_xn, writes=[bpt])
                    if c % 2 == 0:
                        P.op("act", lambda e, c=c, n=n, pt=pt: e.activation(out=vtok[0:n, c, :], in_=pt[0:n, 0:256], func=AF.Copy),
                             reads=[bpt], writes=[b_vtok])
                    else:
                        P.op("dve", lambda e, c=c, n=n, pt=pt: e.tensor_copy(out=vtok[0:n, c, :], in_=pt[0:n, 0:256]),
                             reads=[bpt], writes=[b_vtok])
                sG = load_w(W, l, 0, 16, 5120 + 256 * hd, 256)
                for vc in range(2):
                    kk = 2 * hd + vc
                    proj_fm(psB, b_psB, sG, 128 * vc, 16, xn_rhs, b_xn)
                    for t, (t0, sz) in enumerate(TT):
                        P.op("act", lambda e, t=t, t0=t0, sz=sz, kk=kk: e.activation(
                            out=ymix8[:, kk, t0:t0 + sz], in_=psB[t][:, 0:sz], func=AF.Silu),
                            reads=[b_psB[t]], writes=[b_ymix[kk]])
                sQ = load_w(W, l, 0, 16, 3072 + 128 * hd, 128)
                proj_fm(psA, b_psA, sQ, 0, 16, xn_rhs, b_xn)
                for t, (t0, sz) in enumerate(TT):
                    P.op("dve", lambda e, t=t, t0=t0, sz=sz: e.scalar_tensor_tensor(
                        out=qtl[:, t0:t0 + sz], in0=psA[t][:, 0:sz], scalar=DK ** -0.5, in1=T2buf[:, t0:t0 + sz],
                        op0=ALU.mult, op1=ALU.mult),
                        reads=[b_psA[t], b_TT[1]], writes=[b_qtl])
                P.op("act", lambda e: e.activation(out=T2buf, in_=T1buf, func=AF.Exp, scale=1.0 / 16),
                     reads=[b_TT[0]], writes=[b_TT[1]])
                sK = load_w(W, l, 0, 16, 3584 + 128 * hd, 128)
                proj_fm(psB, b_psB, sK, 0, 16, xn_rhs, b_xn)
                for t, (t0, sz) in enumerate(TT):
                    P.op("dve", lambda e, t=t, t0=t0, sz=sz: e.tensor_tensor(
                        out=ktl[:, t0:t0 + sz], in0=psB[t][:, 0:sz], in1=T2buf[:, t0:t0 + sz], op=ALU.mult),
                        reads=[b_psB[t], b_TT[1]], writes=[b_ktl])
                for c, (s0, n) in enumerate(PCH):
                    P.op("dve", lambda e, c=c, s0=s0, n=n: e.tensor_scalar(
                        out=khT[:, s0:s0 + n], in0=ktl[:, s0:s0 + n], scalar1=Elp[:, c:c + 1], scalar2=None,
                        op0=ALU.mult),
                        reads=[b_ktl, b_El], writes=[b_khT])
                P.op("dve", lambda e: e.tensor_tensor(
                    out=khT[:, NPR:NT].rearrange("p (j i) -> p j i", i=8),
                    in0=ktl[:, NPR:NT].rearrange("p (j i) -> p j i", i=8),
                    in1=Els[:, 0:16].rearrange("p (j o) -> p j o", o=1).to_broadcast([128, 16, 8]),
                    op=ALU.mult),
                    reads=[b_ktl, b_El], writes=[b_khT])
                for c, (s0, n) in enumerate(ACH):
                    P.op("pe", lambda e, s0=s0, n=n: e.matmul(T2[0:n, 256:384], lhsT=khT[:, s0:s0 + n], rhs=ident_bf[:],
                                                               start=True, stop=True),
                         reads=[b_khT, b_identbf], writes=[b_T2b])
                    P.op("act", lambda e, c=c, n=n: e.activation(out=khtok[0:n, c, :], in_=T2[0:n, 256:384], func=AF.Copy),
                         reads=[b_T2b], writes=[b_khtok])
                for c, (s0, n) in enumerate(PCH):
                    P.op("pe", lambda e, c=c, n=n: e.matmul(T1[:, 128:384], lhsT=khtok[0:n, c, :], rhs=vtok[0:n, c, :],
                                                             start=True, stop=True),
                         reads=[b_khtok, b_vtok], writes=[b_T1b])
                    if c == 0:
                        P.op("dve", lambda e: e.tensor_copy(out=pay[:, 0:256], in_=T1[:, 128:384]),
                             reads=[b_T1b], writes=[b_pay])
                    else:
                        P.op("dve", lambda e, c=c: e.scalar_tensor_tensor(
                            out=pay[:, 0:256], in0=pay[:, 0:256], scalar=Elp[:, c:c + 1], in1=T1[:, 128:384],
                            op0=ALU.mult, op1=ALU.add),
                            reads=[b_T1b, b_El], writes=[b_pay])
                b_ccs, b_ccd = Buf(), Buf()
                P.dma_rr("sp", "cx", 2, cc_src[l][hd], pay[:], reads=[b_pay], writes=[b_ccs])
                P.op("pool", lambda e, hd=hd: e.collective_compute(
                    "AllGather", ALU.bypass, replica_groups=[[0, 1], [2, 3], [4, 5], [6, 7]],
                    ins=[cc_src[l][hd]], outs=[cc_dst[l][hd]]), reads=[b_ccs], writes=[b_ccd])
                P.op("pe", lambda e: e.matmul(T1[:, 0:128], lhsT=ktl[:, NPR:NT], rhs=qtl[:, NPR:NT], start=True, stop=True),
                     reads=[b_ktl, b_qtl], writes=[b_T1a])
                P.op("dve", lambda e: e.tensor_tensor(out=sc_sb[0][:], in0=T1[:, 0:128], in1=maskBD[:], op=ALU.mult),
                     reads=[b_T1a, b_maskBD], writes=[b_sc[0]])

                def f(e):
                    ins = None
                    for vc in range(2):
                        ins = e.matmul(T2[:, vc * 128:(vc + 1) * 128], lhsT=vtok[:, 9, vc * 128:(vc + 1) * 128],
                                       rhs=sc_sb[0][:], start=True, stop=True)
                    return ins
                P.op("pe", f, reads=[b_vtok, b_sc[0]], writes=[b_T2a])
                P.op("act", lambda e: e.activation(out=o_sb[:, :, NPR:NT],
                                                   in_=T2[:, 0:256].rearrange("p (v t) -> p v t", v=2), func=AF.Copy),
                     reads=[b_T2a], writes=b_TT)
                P.op("dve", lambda e: e.tensor_tensor(
                    out=km[:], in0=khtok[:, 9, :].rearrange("p (o d) -> p o d", o=1).to_broadcast([128, 16, 128]),
                    in1=MS[:].rearrange("p (j o) -> p j o", o=1).to_broadcast([128, 16, 128]), op=ALU.mult),
                    reads=[b_khtok, b_MS], writes=[b_km])
                for j in range(16):
                    sj = j % 2
                    P.dma_rr("sp", "lds", 2, Sj[sj], sgla_d[l, j, hd], writes=[b_Sj[sj]])
                    P.op("act", lambda e, sj=sj: e.activation(out=Sj_bf[sj], in_=Sj[sj], func=AF.Copy),
                         reads=[b_Sj[sj]], writes=[b_Sjbf[sj]])

                    def f(e, j=j, sj=sj):
                        ins = None
                        for vc in range(2):
                            ins = e.matmul(T2[:, 256 + vc * 128 + 8 * j:256 + vc * 128 + 8 * j + 8],
                                           lhsT=Sj_bf[sj][:, vc * 128:(vc + 1) * 128],
                                           rhs=qtl[:, NPR + 8 * j:NPR + 8 * j + 8], start=True, stop=True)
                        return ins
                    P.op("pe", f, reads=[b_Sjbf[sj], b_qtl], writes=[b_T2b])
                    P.op("pe", lambda e, j=j: e.matmul(T1[:, 128:384], lhsT=km[:, j, :], rhs=vtok[:, 9, :],
                                                        start=True, stop=True),
                         reads=[b_km, b_vtok], writes=[b_T1b])
                    P.op("dve", lambda e, j=j, sj=sj: e.scalar_tensor_tensor(
                        out=Snew[sj], in0=Sj[sj], scalar=Els[:, j:j + 1], in1=T1[:, 128:384], op0=ALU.mult, op1=ALU.add),
                        reads=[b_T1b, b_Sj[sj], b_El], writes=[b_Snew[sj]])
                    note_out(P.dma_rr("sp", "st", 2, gla_s[l, j, hd], Snew[sj], reads=[b_Snew[sj]]))
                P.op("dve", lambda e: e.tensor_tensor(
                    out=o_sb[:, :, NPR:NT], in0=T2[:, 256:512].rearrange("p (v t) -> p v t", v=2),
                    in1=o_sb[:, :, NPR:NT], op=ALU.add),
                    reads=[b_T2b] + b_TT, writes=b_TT)
                P.dma_rr("sp", "cx", 2, sin[:], cc_dst[l][hd][0:128, :], reads=[b_ccd], writes=[b_sin])
                P.op("dve", lambda e: e.tensor_scalar(out=S32, in0=sin[:, 0:256], scalar1=role[:, 0:1], scalar2=None,
                                                      op0=ALU.mult),
                     reads=[b_sin, b_role], writes=[b_S])
                P.op("act", lambda e: e.activation(out=S_bf, in_=S32, func=AF.Copy), reads=[b_S], writes=[b_Sbf])
                if hd == 0:
                    P.op("dve", lambda e: e.tensor_scalar(out=halo_in, in0=sin[:, 256:272], scalar1=role[:, 0:1],
                                                          scalar2=None, op0=ALU.mult),
                         reads=[b_sin, b_role], writes=[b_halo])
                for c, (s0, n) in enumerate(PCH):
                    si = c % 2
                    P.op("pe", lambda e, s0=s0, n=n: e.matmul(T1[0:n, 0:n], lhsT=ktl[:, s0:s0 + n], rhs=qtl[:, s0:s0 + n],
                                                               start=True, stop=True),
                         reads=[b_ktl, b_qtl], writes=[b_T1a])
                    P.op("dve", lambda e, n=n, si=si: e.tensor_tensor(out=sc_sb[si][0:n, 0:n], in0=T1[0:n, 0:n],
                                                                      in1=maskU[0:n, 0:n], op=ALU.mult),
                         reads=[b_T1a, b_maskU], writes=[b_sc[si]])

                    def f(e, c=c, s0=s0, n=n, si=si):
                        ins = None
                        for vc in range(2):
                            e.matmul(T2[:, vc * 128:vc * 128 + n], lhsT=S_bf[:, vc * 128:(vc + 1) * 128],
                                     rhs=qtl[:, s0:s0 + n], start=True, stop=False)
                            ins = e.matmul(T2[:, vc * 128:vc * 128 + n], lhsT=vtok[0:n, c, vc * 128:(vc + 1) * 128],
                                           rhs=sc_sb[si][0:n, 0:n], start=False, stop=True)
                        return ins
                    P.op("pe", f, reads=[b_Sbf, b_qtl, b_vtok, b_sc[si]], writes=[b_T2a])
                    P.op("act", lambda e, s0=s0, n=n: e.activation(
                        out=o_sb[:, :, s0:s0 + n], in_=T2[:, 0:256].rearrange("p (v t) -> p v t", v=2)[:, :, 0:n],
                        func=AF.Copy), reads=[b_T2a], writes=b_TT)
                    P.op("pe", lambda e, c=c, n=n: e.matmul(T1[:, 128:384], lhsT=khtok[0:n, c, :], rhs=vtok[0:n, c, :],
                                                             start=True, stop=True),
                         reads=[b_khtok, b_vtok], writes=[b_T1b])
                    P.op("dve", lambda e, c=c: e.scalar_tensor_tensor(
                        out=S32, in0=S32, scalar=Elp[:, c:c + 1], in1=T1[:, 128:384], op0=ALU.mult, op1=ALU.add),
                        reads=[b_T1b, b_El], writes=[b_S])
                    if c < len(PCH) - 1:
                        P.op("act", lambda e: e.activation(out=S_bf, in_=S32, func=AF.Copy), reads=[b_S], writes=[b_Sbf])
                note_out(P.dma_rr("sp", "st", 2, gla_p[l, hd], S32, reads=[b_S]))
                for vc in range(2):
                    P.op("act", lambda e, vc=vc: e.activation(out=sq[vc][:], in_=o_sb[:, vc, :], func=AF.Square),
                         reads=[b_TT[vc]], writes=[b_sq[vc]])
                    ones_mm(psA, b_psA, sq[vc], b_sq[vc], vc == 0, vc == 1)
                stats_to_rstd(psA, b_psA, 1.0 / 256)
                for vc in range(2):
                    kk = 2 * hd + vc
                    P.op("dve", lambda e, vc=vc: e.scalar_tensor_tensor(
                        out=o_sb[:, vc, :], in0=o_sb[:, vc, :], scalar=vecs[:, vb + 84 + vc:vb + 85 + vc], in1=rstd[:],
                        op0=ALU.mult, op1=ALU.mult),
                        reads=[b_TT[vc], b_vecs, b_rstd], writes=[b_TT[vc]])
                    P.op("dve", lambda e, vc=vc, kk=kk: e.tensor_tensor(
                        out=ymix8[:, kk, :], in0=o_sb[:, vc, :], in1=ymix8[:, kk, :], op=ALU.mult),
                        reads=[b_TT[vc], b_ymix[kk]], writes=[b_ymix[kk]])

            def wout_half(row0):
                for blk in range(8):
                    sO = load_w("wout", l, row0, 8, 256 * blk, 256)
                    for i in range(2):
                        dc = 2 * blk + i
                        pset, bset = (psA, b_psA) if dc % 2 == 0 else (psB, b_psB)
                        proj_fm(pset, bset, sO, 128 * i, 8, lambda k, t0, sz: ymix8[:, k, t0:t0 + sz], b_ymix)
                        for t, (t0, sz) in enumerate(TT):
                            P.op("dve", lambda e, t=t, t0=t0, sz=sz, dc=dc, pset=pset: e.tensor_tensor(
                                out=h[:, dc, t0:t0 + sz], in0=pset[t][:, 0:sz], in1=h[:, dc, t0:t0 + sz], op=ALU.add),
                                reads=[bset[t], b_h[dc]], writes=[b_h[dc]])

            wout_half(1024)

            barrier(G_gla32 + G_conv32)
            for half in range(2):
                P.dma_rr("sp", "ldc", 1, stg[0:32, :], sconv_d[l, :, 512 * half:512 * (half + 1)], writes=[b_stg])
                for c4 in range(4):
                    cc = 4 * half + c4
                    P.op("pe", lambda e, c4=c4: e.transpose(T2[:, 256:288], stg[0:32, c4 * 128:(c4 + 1) * 128],
                                                            ident[0:32, 0:32]),
                         reads=[b_stg, b_ident], writes=[b_T2b])
                    P.op("act", lambda e, cc=cc: e.activation(out=sconvT[:, cc, :], in_=T2[:, 256:288], func=AF.Copy),
                         reads=[b_T2b], writes=[b_sconvT])
            upad_s = upad[:, 1034:1194].rearrange("p (j i) -> p j i", i=10)
            for cp in range(4):
                sB_ = load_w(W, l, 0, 16, 256 * cp, 256)
                sC_ = load_w(W, l, 0, 16, 1024 + 256 * cp, 256)
                sH_ = load_w(W, l, 0, 16, 2048 + 256 * cp, 256)
                for i in range(2):
                    cc = 2 * cp + i
                    proj_fm(psA, b_psA, sC_, 128 * i, 16, xn_rhs, b_xn)
                    for t, (t0, sz) in enumerate(TT):
                        P.op("act", lambda e, t=t, t0=t0, sz=sz: e.activation(out=cC_sb[:, t0:t0 + sz], in_=psA[t][:, 0:sz],
                                                                             func=AF.Copy),
                             reads=[b_psA[t]], writes=[b_cC])
                    proj_fm(psB, b_psB, sH_, 128 * i, 16, xn_rhs, b_xn)
                    for t, (t0, sz) in enumerate(TT[:2]):
                        P.op("dve", lambda e, t=t, t0=t0, sz=sz: e.tensor_tensor(
                            out=upad[:, 2 + t0:2 + t0 + sz], in0=psB[t][:, 0:sz], in1=cC_sb[:, t0:t0 + sz], op=ALU.mult),
                            reads=[b_psB[t], b_cC], writes=[b_upad])
                    P.op("dve", lambda e: e.tensor_tensor(
                        out=upad[:, 778:1034], in0=psB[2][:, 0:256], in1=cC_sb[:, 776:1032], op=ALU.mult),
                        reads=[b_psB[2], b_cC], writes=[b_upad])
                    P.op("dve", lambda e: e.tensor_tensor(
                        out=upad_s[:, :, 2:10], in0=psB[2][:, 256:384].rearrange("p (j i) -> p j i", i=8),
                        in1=cC_sb[:, NPR:NT].rearrange("p (j i) -> p j i", i=8), op=ALU.mult),
                        reads=[b_psB[2], b_cC], writes=[b_upad])
                    P.op("act", lambda e, cc=cc: e.activation(out=upad[:, 0:2], in_=halo_in[:, 2 * cc:2 * cc + 2], func=AF.Copy),
                         reads=[b_halo], writes=[b_upad])
                    P.op("act", lambda e, cc=cc: e.activation(
                        out=upad_s[:, :, 0:2], in_=sconvT[:, cc, :].rearrange("p (j r) -> p j r", r=2), func=AF.Copy),
                        reads=[b_sconvT], writes=[b_upad])
                    P.op("act", lambda e, cc=cc: e.activation(out=uo[:, cc, 0:2], in_=upad[:, 1032:1034], func=AF.Copy),
                         reads=[b_upad], writes=[b_uo])
                    P.op("act", lambda e, cc=cc: e.activation(
                        out=uo[:, cc, 2:34].rearrange("p (j r) -> p j r", r=2), in_=upad_s[:, :, 8:10], func=AF.Copy),
                        reads=[b_upad], writes=[b_uo])
                    proj_fm(psA, b_psA, sB_, 128 * i, 16, xn_rhs, b_xn)
                    cw = vb + 48
                    P.op("dve", lambda e, cc=cc: e.tensor_scalar(
                        out=cv[:, 0:1192], in0=upad[:, 0:1192], scalar1=vecs[:, cw + cc:cw + cc + 1], scalar2=None,
                        op0=ALU.mult), reads=[b_upad, b_vecs], writes=[b_cv])
                    for jj in (1, 2):
                        P.op("dve", lambda e, cc=cc, jj=jj: e.scalar_tensor_tensor(
                            out=cv[:, 0:1192], in0=upad[:, jj:jj + 1192],
                            scalar=vecs[:, cw + 8 * jj + cc:cw + 8 * jj + cc + 1], in1=cv[:, 0:1192],
                            op0=ALU.mult, op1=ALU.add), reads=[b_upad, b_vecs, b_cv], writes=[b_cv])
                    for t, (t0, sz) in enumerate(TT[:2]):
                        P.op("dve", lambda e, t=t, t0=t0, sz=sz: e.tensor_tensor(
                            out=zt[:, t0:t0 + sz], in0=psA[t][:, 0:sz], in1=cv[:, t0:t0 + sz], op=ALU.mult),
                            reads=[b_psA[t], b_cv], writes=[b_z])
                    P.op("dve", lambda e: e.tensor_tensor(
                        out=zt[:, 776:1032], in0=psA[2][:, 0:256], in1=cv[:, 776:1032], op=ALU.mult),
                        reads=[b_psA[2], b_cv], writes=[b_z])
                    P.op("dve", lambda e: e.tensor_tensor(
                        out=zt[:, NPR:NT].rearrange("p (j i) -> p j i", i=8),
                        in0=psA[2][:, 256:384].rearrange("p (j i) -> p j i", i=8),
                        in1=cv[:, 1034:1194].rearrange("p (j i) -> p j i", i=10)[:, :, 0:8], op=ALU.mult),
                        reads=[b_psA[2], b_cv], writes=[b_z])
                    P.op("act", lambda e: e.activation(out=sq[0][:], in_=zt, func=AF.Square), reads=[b_z], writes=[b_sq[0]])
                    ones_mm(psB, b_psB, sq[0], b_sq[0], True, True)
                    stats_to_rstd(psB, b_psB, 1.0 / 128)
                    P.op("dve", lambda e, cc=cc: e.scalar_tensor_tensor(
                        out=ymix8[:, cc, :], in0=zt, scalar=vecs[:, vb + 72 + cc:vb + 73 + cc], in1=rstd[:],
                        op0=ALU.mult, op1=ALU.mult),
                        reads=[b_z, b_vecs, b_rstd], writes=[b_ymix[cc]])
            for half in range(2):
                def f(e, half=half):
                    ins = None
                    for c4 in range(4):
                        cc = 4 * half + c4
                        ins = e.transpose(T1[0:34, c4 * 128:(c4 + 1) * 128], uo[:, cc, :], ident[:])
                    return ins
                P.op("pe", f, reads=[b_uo, b_ident], writes=[b_T1a, b_T1b])
                P.op("act", lambda e: e.activation(out=stg[0:34, :], in_=T1[0:34, :], func=AF.Copy),
                     reads=[b_T1a, b_T1b], writes=[b_stg])
                note_out(P.dma_rr("sp", "st", 2, conv_p[l, :, 512 * half:512 * (half + 1)], stg[0:2, :], reads=[b_stg]))
                note_out(P.dma_rr("sp", "st", 2, conv_s[l, :, 512 * half:512 * (half + 1)], stg[2:34, :], reads=[b_stg]))
            wout_half(0)

        for l in range(depth):
            ffn(l, 0, NVL * l + 0)
            mixer(l)
            ffn(l, 1, NVL * l + 32)

        if dbg:
            note_out(P.dma("sp", "stdbg", dbg_h, h[:], reads=b_h))

        barrier(G_all)
        h_stats()
        gf = NVL * DEPTH
        for kc in range(16):
            P.op("dve", lambda e, kc=kc: e.scalar_tensor_tensor(
                out=h[:, kc, :], in0=h[:, kc, :], scalar=vecs[:, gf + kc:gf + kc + 1], in1=rstd[:],
                op0=ALU.mult, op1=ALU.mult),
                reads=[b_h[kc], b_vecs, b_rstd], writes=[b_h[kc]])
        ytok = xtok
        b_ytok = b_xtok
        gi = 0
        for c, (c0, n) in enumerate(XCH):
            s = c % 2
            for g in range(4):
                Tt, bT = Tb[gi % 2]

                def f(e, g=g, c0=c0, n=n, Tt=Tt):
                    ins = None
                    for j in range(4):
                        kc = 4 * g + j
                        ins = e.transpose(Tt[0:n, j * 128:(j + 1) * 128], h[:, kc, c0:c0 + n], ident[:])
                    return ins
                P.op("pe", f, reads=b_h[4 * g:4 * g + 4] + [b_ident], writes=bT)
                if gi % 2 == 0:
                    P.op("dve", lambda e, s=s, g=g, n=n, Tt=Tt: e.tensor_copy(out=ytok[s][0:n, 512 * g:512 * (g + 1)],
                                                                             in_=Tt[0:n, :]),
                         reads=bT, writes=[b_ytok[s]])
                else:
                    P.op("act", lambda e, s=s, g=g, n=n, Tt=Tt: e.activation(out=ytok[s][0:n, 512 * g:512 * (g + 1)],
                                                                            in_=Tt[0:n, :], func=AF.Copy),
                         reads=bT, writes=[b_ytok[s]])
                gi += 1
            note_out(P.dma_rr("sp", "sty", 2, y_out[c0:c0 + n, :], ytok[s][0:n, :], reads=[b_ytok[s]]))

        P.wait_all("sp", list(out_toks.items()))
        P.build()
    return nc


def _cols(v):
    v = np.asarray(v, np.float32)
    return np.ascontiguousarray(v.reshape(-1, 128).T)


_NC_CACHE = {}


def make_in_maps(x_prompt, x_sample, state_conv, state_gla, meta_tokens, norm_ffn1, w_ffn1_gu,
           w_ffn1_down, norm_mix, w_mix_in, conv_w, conv_norm, gla_fgate_w2, gla_fgate_b,
           gla_out_norm, w_mix_out, norm_ffn2, w_ffn2_gu, w_ffn2_down, norm_final, depth=DEPTH):
    f32 = np.float32
    x_prompt = np.asarray(x_prompt, f32)
    x_sample = np.asarray(x_sample, f32)
    state_conv = np.asarray(state_conv, f32)
    state_gla = np.asarray(state_gla, f32)
    meta_tokens = np.asarray(meta_tokens, f32)
    ws = {
        "wgu1": np.asarray(w_ffn1_gu, f32), "wdn1": np.asarray(w_ffn1_down, f32),
        "win": np.asarray(w_mix_in, f32), "wout": np.asarray(w_mix_out, f32),
        "wgu2": np.asarray(w_ffn2_gu, f32), "wdn2": np.asarray(w_ffn2_down, f32),
    }
    vecs = np.zeros((128, NV), f32)
    for l in range(DEPTH):
        b = NVL * l
        vecs[:, b + 0:b + 16] = _cols(norm_ffn1[l])
        vecs[:, b + 16:b + 32] = _cols(norm_mix[l])
        vecs[:, b + 32:b + 48] = _cols(norm_ffn2[l])
        for j in range(3):
            vecs[:, b + 48 + 8 * j:b + 56 + 8 * j] = _cols(np.asarray(conv_w)[l, j])
        vecs[:, b + 72:b + 80] = _cols(conv_norm[l])
        vecs[:, b + 80:b + 84] = _cols(gla_fgate_b[l])
        vecs[:, b + 84:b + 86] = _cols(gla_out_norm[l])
    vecs[:, NVL * DEPTH:NVL * DEPTH + 16] = _cols(norm_final)
    fw2 = np.ascontiguousarray(np.asarray(gla_fgate_w2, f32))

    in_maps = []
    for c in range(8):
        p, r = c // 2, c % 2
        if r == 0:
            xp = np.concatenate([meta_tokens, x_prompt[p, 0:1016]], axis=0)
        else:
            xp = x_prompt[p, 1016:2048]
        xs = x_sample[16 * c:16 * (c + 1)].reshape(128, D)
        role = np.zeros((128, 8), f32)
        role[:, 0] = float(r)
        if r == 1:
            role[:, 1 + p] = 1.0
        m = {
            "x_in": np.ascontiguousarray(np.concatenate([xp, xs], axis=0)),
            "vecs": vecs, "role": role, "fw2": fw2,
            "sconv": np.ascontiguousarray(state_conv[:, 16 * c:16 * (c + 1)].reshape(DEPTH, 32, 1024)),
            "sgla": np.ascontiguousarray(state_gla[:, 16 * c:16 * (c + 1)]),
        }
        for name, w in ws.items():
            m[name] = w[:depth]
        in_maps.append(m)

    return in_maps


def kernel(x_prompt, x_sample, state_conv, state_gla, meta_tokens, norm_ffn1, w_ffn1_gu,
           w_ffn1_down, norm_mix, w_mix_in, conv_w, conv_norm, gla_fgate_w2, gla_fgate_b,
           gla_out_norm, w_mix_out, norm_ffn2, w_ffn2_gu, w_ffn2_down, norm_final):
    f32 = np.float32
    in_maps = make_in_maps(x_prompt, x_sample, state_conv, state_gla, meta_tokens, norm_ffn1, w_ffn1_gu,
                           w_ffn1_down, norm_mix, w_mix_in, conv_w, conv_norm, gla_fgate_w2, gla_fgate_b,
                           gla_out_norm, w_mix_out, norm_ffn2, w_ffn2_gu, w_ffn2_down, norm_final)
    if "nc" not in _NC_CACHE:
        _NC_CACHE["nc"] = build_program()
    nc = _NC_CACHE["nc"]
    res = run_bass_kernel_spmd(nc, in_maps, core_ids=list(range(8)))
    R = res.results

    y_prompt = np.empty((4, 2048, D), f32)
    y_sample = np.empty((128, 8, D), f32)
    new_gla_prompt = np.empty((DEPTH, 4, 4, 128, 256), f32)
    new_conv_prompt = np.empty((DEPTH, 4, 2, 1024), f32)
    new_gla_sample = np.empty((DEPTH, 128, 4, 128, 256), f32)
    new_conv_sample = np.empty((DEPTH, 128, 2, 1024), f32)
    for c in range(8):
        p, r = c // 2, c % 2
        y = R[c]["y_out"]
        if r == 0:
            y_prompt[p, 0:1016] = y[16:1032]
        else:
            y_prompt[p, 1016:2048] = y[0:1032]
            new_gla_prompt[:, p] = R[c]["gla_p"]
            new_conv_prompt[:, p] = R[c]["conv_p"]
        y_sample[16 * c:16 * (c + 1)] = y[1032:1160].reshape(16, 8, D)
        new_gla_sample[:, 16 * c:16 * (c + 1)] = R[c]["gla_s"]
        new_conv_sample[:, 16 * c:16 * (c + 1)] = R[c]["conv_s"].reshape(DEPTH, 16, 2, 1024)
    return (y_prompt, y_sample, new_gla_prompt, new_conv_prompt, new_gla_sample, new_conv_sample)
```
